# Optimizing a Trainium2 kernel written in Bass

```python
import jax, jax.numpy as jnp
from jax import lax
import numpy as np

D_MODEL = 1024
BATCH = 4
SEQ = 4096
DEPTH = 2

GRID_W = 64
CTX_LEN = 256
D_LRU = 1024
LRU_HEADS = 16
LRU_HEAD_DIM = D_LRU // LRU_HEADS
CONV_WIDTH = 4
CONV_PAD = (2, 1)
LRU_C = 8.0
D_POOL = 512
POOL_WINDOWS = (2, 4, 8, 16)
POOL_GROUP = D_POOL // len(POOL_WINDOWS)
D_FF = 4 * D_MODEL
N_BRANCH = 2
D_IN = 2 * D_LRU + D_POOL + N_BRANCH * D_MODEL
IN_SPLITS = (D_LRU, 2 * D_LRU, 2 * D_LRU + D_POOL)
N_MOD = 6
EPS = 1e-6

kernel_name = "hybrid_rglru_pool_dit_block"


def rms_norm(x, g):
    xf = x.astype(jnp.float32)
    y = xf * lax.rsqrt(jnp.mean(xf * xf, axis=-1, keepdims=True) + EPS)
    return (y * g.astype(jnp.float32)).astype(x.dtype)


def modulate(x, g, shift, scale):
    return rms_norm(x, g) * (1.0 + scale[:, None, :]) + shift[:, None, :]


def depthwise_conv(u, w, b):
    y = lax.conv_general_dilated(
        u, w[:, None, :].astype(u.dtype), window_strides=(1,), padding=[CONV_PAD],
        dimension_numbers=('NWC', 'WIO', 'NWC'), feature_group_count=u.shape[-1])
    return y + b


def rglru_coeffs(uc, w_r, b_r, w_i, b_i, lam):
    bsz, L, _ = uc.shape
    uh = uc.reshape(bsz, L, LRU_HEADS, LRU_HEAD_DIM)
    r = jax.nn.sigmoid(jnp.einsum('blhd,hde->blhe', uh, w_r).reshape(bsz, L, D_LRU) + b_r)
    i = jax.nn.sigmoid(jnp.einsum('blhd,hde->blhe', uh, w_i).reshape(bsz, L, D_LRU) + b_i)
    log_a = -LRU_C * r.astype(jnp.float32) * jax.nn.softplus(-lam.astype(jnp.float32))
    a = jnp.exp(log_a)
    b = jnp.sqrt(-jnp.expm1(2.0 * log_a)) * (i * uc).astype(jnp.float32)
    return a, b


def linear_scan(a, b, h0, reverse):
    if reverse:
        a, b = jnp.flip(a, axis=1), jnp.flip(b, axis=1)
    b = b.at[:, 0].add(a[:, 0] * h0)

    def combine(e1, e2):
        a1, b1 = e1
        a2, b2 = e2
        return a1 * a2, a2 * b1 + b2

    _, h = lax.associative_scan(combine, (a, b), axis=1)
    if reverse:
        h = jnp.flip(h, axis=1)
    return h


def rglru_bidirectional(u_lat, u_ctx, conv_w, conv_b, w_r, b_r, w_i, b_i, lam):
    uc_l = depthwise_conv(u_lat, conv_w, conv_b)
    uc_c = depthwise_conv(u_ctx, conv_w, conv_b)
    h0 = jnp.zeros((u_ctx.shape[0], D_LRU), jnp.float32)
    out_l = jnp.zeros(uc_l.shape, jnp.float32)
    out_c = jnp.zeros(uc_c.shape, jnp.float32)
    for d, rev in enumerate((False, True)):
        a_c, b_c = rglru_coeffs(uc_c, w_r[d], b_r[d], w_i[d], b_i[d], lam[d])
        h_c = linear_scan(a_c, b_c, h0, rev)
        h_final = h_c[:, 0] if rev else h_c[:, -1]
        a_l, b_l = rglru_coeffs(uc_l, w_r[d], b_r[d], w_i[d], b_i[d], lam[d])
        out_l = out_l + linear_scan(a_l, b_l, h_final, rev)
        out_c = out_c + h_c
    return out_l.astype(u_lat.dtype), out_c.astype(u_ctx.dtype)


def pool_mixer(p, pool_w, pool_scale):
    L = p.shape[-2]
    pf = p.astype(jnp.float32)
    cs = jnp.concatenate([jnp.zeros_like(pf[..., :1, :]), jnp.cumsum(pf, axis=-2)], axis=-2)
    t = jnp.arange(L)
    groups = []
    for gi, w in enumerate(POOL_WINDOWS):
        sl = slice(gi * POOL_GROUP, (gi + 1) * POOL_GROUP)
        lo = jnp.clip(t - w // 2, 0, L)
        hi = jnp.clip(t + w - w // 2, 0, L)
        csg = cs[..., sl]
        mean = (jnp.take(csg, hi, axis=-2) - jnp.take(csg, lo, axis=-2)) / (hi - lo).astype(jnp.float32)[:, None]
        groups.append(mean - pf[..., sl])
    m = jnp.stack(groups, axis=-2)
    y = jnp.einsum('bnlgc,gce->bnlge', m.astype(p.dtype), pool_w)
    return y.reshape(p.shape) * pool_scale


def branch_merge(h_lru, g, p_mixed, gt, w_lru_out, w_pool_out, w_o):
    lru_out = (h_lru * jax.nn.gelu(g)) @ w_lru_out
    pool_out = p_mixed @ w_pool_out
    g_lru, g_pool = jnp.split(jax.nn.sigmoid(gt), N_BRANCH, axis=-1)
    return (g_lru * lru_out + g_pool * pool_out) @ w_o


def sq_relu_mlp(h, w1, w2):
    return jnp.square(jax.nn.relu(h @ w1)) @ w2


def setup_inputs(seed: int = 0) -> dict:
    key = jax.random.key(seed)
    ks = jax.random.split(key, 24)
    f32 = jnp.float32
    D, L_ = D_MODEL, DEPTH

    def nrm(k, shape, scale):
        return jax.random.normal(k, shape, f32) * scale

    u = jax.random.uniform(ks[15], (L_, 2, D_LRU), f32, minval=0.9, maxval=0.999)
    return {
        "x": nrm(ks[0], (BATCH, SEQ, D), 1.0),
        "c": nrm(ks[1], (BATCH, D), 1.0),
        "ctx": nrm(ks[2], (BATCH, CTX_LEN, D), 1.0),
        "c_ctx": nrm(ks[3], (D,), 1.0),
        "w_ada": nrm(ks[4], (L_, D, N_MOD * D), 0.5 * D ** -0.5),
        "b_ada": nrm(ks[5], (L_, N_MOD * D), 0.02),
        "norm1_g": 1.0 + nrm(ks[6], (L_, D), 0.05),
        "norm2_g": 1.0 + nrm(ks[7], (L_, D), 0.05),
        "w_in": nrm(ks[8], (L_, D, D_IN), D ** -0.5),
        "conv_w": nrm(ks[9], (L_, CONV_WIDTH, D_LRU), CONV_WIDTH ** -0.5),
        "conv_b": nrm(ks[10], (L_, D_LRU), 0.02),
        "lru_w_r": nrm(ks[11], (L_, 2, LRU_HEADS, LRU_HEAD_DIM, LRU_HEAD_DIM), LRU_HEAD_DIM ** -0.5),
        "lru_b_r": nrm(ks[12], (L_, 2, D_LRU), 0.02),
        "lru_w_i": nrm(ks[13], (L_, 2, LRU_HEADS, LRU_HEAD_DIM, LRU_HEAD_DIM), LRU_HEAD_DIM ** -0.5),
        "lru_b_i": nrm(ks[14], (L_, 2, D_LRU), 0.02),
        "lru_lambda": jnp.log(u) - jnp.log1p(-u),
        "w_lru_out": nrm(ks[16], (L_, D_LRU, D), D_LRU ** -0.5),
        "pool_w": nrm(ks[17], (L_, len(POOL_WINDOWS), POOL_GROUP, POOL_GROUP), POOL_GROUP ** -0.5),
        "pool_scale": 1.0 + nrm(ks[18], (L_, D_POOL), 0.1),
        "w_pool_out": nrm(ks[19], (L_, D_POOL, D), D_POOL ** -0.5),
        "w_o": nrm(ks[20], (L_, D, D), D ** -0.5),
        "mlp_w1": nrm(ks[21], (L_, D, D_FF), D ** -0.5),
        "mlp_w2": nrm(ks[22], (L_, D_FF, D), D_FF ** -0.5),
        "final_g": 1.0 + nrm(ks[23], (D,), 0.05),
    }


def reference(x, c, ctx, c_ctx, w_ada, b_ada, norm1_g, norm2_g, w_in, conv_w, conv_b,
              lru_w_r, lru_b_r, lru_w_i, lru_b_i, lru_lambda, w_lru_out, pool_w, pool_scale,
              w_pool_out, w_o, mlp_w1, mlp_w2, final_g):
    bsz, seq, _ = x.shape
    rows = seq // GRID_W
    silu_c = jax.nn.silu(c)
    silu_cc = jax.nn.silu(c_ctx)[None]
    for l in range(DEPTH):
        last = l == DEPTH - 1
        shift1, scale1, gate1, shift2, scale2, gate2 = jnp.split(silu_c @ w_ada[l] + b_ada[l], N_MOD, axis=-1)
        cshift1, cscale1, cgate1, cshift2, cscale2, cgate2 = jnp.split(silu_cc @ w_ada[l] + b_ada[l], N_MOD, axis=-1)

        h_l = modulate(x, norm1_g[l], shift1, scale1)
        h_c = modulate(ctx, norm1_g[l], cshift1, cscale1)
        u_l, g_l, p_l, gt_l = jnp.split(h_l @ w_in[l], IN_SPLITS, axis=-1)
        if last:
            u_c = h_c @ w_in[l][:, :D_LRU]
        else:
            u_c, g_c, p_c, gt_c = jnp.split(h_c @ w_in[l], IN_SPLITS, axis=-1)
        hl_lru, hc_lru = rglru_bidirectional(u_l, u_c, conv_w[l], conv_b[l], lru_w_r[l], lru_b_r[l],
                                             lru_w_i[l], lru_b_i[l], lru_lambda[l])
        pm_l = pool_mixer(p_l.reshape(bsz, rows, GRID_W, D_POOL), pool_w[l], pool_scale[l]).reshape(bsz, seq, D_POOL)
        y_l = branch_merge(hl_lru, g_l, pm_l, gt_l, w_lru_out[l], w_pool_out[l], w_o[l])
        x = x + gate1[:, None, :] * y_l
        x = x + gate2[:, None, :] * sq_relu_mlp(modulate(x, norm2_g[l], shift2, scale2), mlp_w1[l], mlp_w2[l])

        if not last:
            pm_c = pool_mixer(p_c[:, None], pool_w[l], pool_scale[l])[:, 0]
            y_c = branch_merge(hc_lru, g_c, pm_c, gt_c, w_lru_out[l], w_pool_out[l], w_o[l])
            ctx = ctx + cgate1[:, None, :] * y_c
            ctx = ctx + cgate2[:, None, :] * sq_relu_mlp(modulate(ctx, norm2_g[l], cshift2, cscale2), mlp_w1[l], mlp_w2[l])
    return rms_norm(x, final_g)
```

```python
import numpy as np
from contextlib import ExitStack
import concourse.bass as bass
import concourse.mybir as mybir
from concourse.bass_utils import run_bass_kernel_spmd

F32 = mybir.dt.float32
BF16 = mybir.dt.bfloat16
AF = mybir.ActivationFunctionType
ALU = mybir.AluOpType

D = 1024
NCH = 8
NB = 4
SEQ = 4096
T = SEQ // 2
TC = 256
DEPTH = 2
D_IN = 4608
D_FF = 4096
EPS = 1e-6
NT = 512
SEM_CAP = 4000
UW = T + 4

V_N1G, V_N2G, V_CB, V_CW, V_BR, V_BI, V_LAM, V_PS, V_BADA = 0, 8, 16, 24, 64, 80, 96, 112, 116
V_PER_LAYER = 164
VG_FG, VG_C, VG_CC, VG_SEL = 0, 8, 16, 24
V_GLOBAL = 26


class Op:
    __slots__ = ("eng", "fn", "deps", "dma_sem", "token", "needed", "inc")

    def __init__(self, eng, fn, dma_sem, inc):
        self.eng, self.fn, self.dma_sem, self.inc = eng, fn, dma_sem, inc
        self.deps = set()
        self.token = None
        self.needed = False


class Prog:
    ENGS = ("pe", "act", "dve", "pool", "sp")

    def __init__(self):
        self.ops = []
        self.last_w = {}
        self.readers = {}
        self.dma_sem_names = []

    def new_dma_sem(self, name):
        self.dma_sem_names.append(name)
        return name

    def add(self, eng, fn, reads=(), writes=(), dma_sem=None, inc=None, waw=True):
        o = Op(eng, fn, dma_sem, inc if inc is not None else (16 if dma_sem else 1))
        for k in reads:
            w = self.last_w.get(k)
            if w is not None:
                o.deps.add(w)
        for k in writes:
            w = self.last_w.get(k)
            if w is not None and waw:
                o.deps.add(w)
            for r in self.readers.get(k, ()):
                o.deps.add(r)
        o.deps.discard(o)
        for k in reads:
            self.readers.setdefault(k, []).append(o)
        for k in writes:
            self.last_w[k] = o
            self.readers[k] = []
        self.ops.append(o)
        return o

    def emit(self, nc, stack):
        for o in self.ops:
            for d in o.deps:
                d.needed = True
        cnt = {e: 0 for e in self.ENGS}
        dcnt = {}
        for o in self.ops:
            if o.dma_sem is not None:
                dcnt[o.dma_sem] = dcnt.get(o.dma_sem, 0) + o.inc
                o.token = (o.dma_sem, dcnt[o.dma_sem])
            elif o.needed:
                cnt[o.eng] += 1
                k = cnt[o.eng]
                o.token = ((o.eng, (k - 1) // SEM_CAP), (k - 1) % SEM_CAP + 1)
        sems = {}
        for e in self.ENGS:
            for ep in range((cnt[e] + SEM_CAP - 1) // SEM_CAP):
                sems[(e, ep)] = stack.enter_context(nc.semaphore(f"s_{e}{ep}"))
        for n in self.dma_sem_names:
            if n in dcnt:
                sems[n] = stack.enter_context(nc.semaphore(f"d_{n}"))
        block = stack.enter_context(nc.Block())
        per_eng = {e: [o for o in self.ops if o.eng == e] for e in self.ENGS}

        def run(eng_name, eng):
            waited = {}
            for o in per_eng[eng_name]:
                need = {}
                for d in o.deps:
                    if d.dma_sem is None and d.eng == "pe" and eng_name == "pe":
                        continue
                    key, val = d.token
                    if need.get(key, 0) < val:
                        need[key] = val
                for key, val in need.items():
                    if waited.get(key, 0) >= val:
                        continue
                    eng.wait_ge(sems[key], val)
                    waited[key] = val
                ins = o.fn(eng)
                if o.token is not None:
                    if o.dma_sem is not None and o.inc == 1:
                        ins.then_inc(sems[o.token[0]])
                    else:
                        ins.then_inc(sems[o.token[0]], o.inc)

        @block.tensor
        def _(e):
            run("pe", e)

        @block.scalar
        def _(e):
            run("act", e)

        @block.vector
        def _(e):
            run("dve", e)

        @block.gpsimd
        def _(e):
            run("pool", e)

        @block.sync
        def _(e):
            run("sp", e)


def build_nc(layers=(0, 1), final=True, stop=None, debug=False, noexch=False):
    nc = bass.Bass("TRN2", target_bir_lowering=False)
    P = Prog()
    st = ExitStack()
    nl = len(layers)

    class _Stop(Exception):
        pass

    def chk(name):
        if stop == name:
            raise _Stop()

    def din(name, shape):
        return nc.dram_tensor(name, list(shape), F32, kind="ExternalInput").ap()

    x_in = din("x_in", [T, D])
    c_in = din("c_in", [TC, D])
    vec_g = din("vec_g", [128, V_GLOBAL])
    vec_l = din("vec_l", [DEPTH, 128, V_PER_LAYER])
    ident_d = din("ident", [128, 128])
    pml_d = din("pm_lat", [128, 4, 128])
    pmc_d = din("pm_ctx", [128, 4, 2, 256])
    gw_d = din("gw", [DEPTH, 128, 4, 8, 128])
    poolw_d = din("pool_w", [DEPTH, 128, 4, 128])
    w_ada_d = din("w_ada", [DEPTH, D, 6 * D])
    w_in_d = din("w_in", [DEPTH, D, D_IN])
    w_lo_d = din("w_lru_out", [DEPTH, D, D])
    w_po_d = din("w_pool_out", [DEPTH, 512, D])
    w_o_d = din("w_o", [DEPTH, D, D])
    w1_d = din("mlp_w1", [DEPTH, D, D_FF])
    w2_d = din("mlp_w2", [DEPTH, D_FF, D])
    out_d = nc.dram_tensor("out", [T, D], F32, kind="ExternalOutput").ap()
    outc_d = None
    if not final:
        outc_d = nc.dram_tensor("out_ctx", [TC, D], F32, kind="ExternalOutput").ap()

    def dscr(name, shape, dt):
        if debug and name.startswith("sp_"):
            return nc.dram_tensor(name, list(shape), dt, kind="ExternalOutput").ap()
        return nc.dram_tensor(name, list(shape), dt).ap()

    wsc = {}
    for l in layers:
        wsc[("win", l)] = dscr(f"s_win{l}", [18, 128, 8, 256], BF16)
        wsc[("wlo", l)] = dscr(f"s_wlo{l}", [4, 128, 8, 256], BF16)
        wsc[("wo", l)] = dscr(f"s_wo{l}", [4, 128, 8, 256], BF16)
        wsc[("wpo", l)] = dscr(f"s_wpo{l}", [2, 128, 4, 512], BF16)
        wsc[("w1", l)] = dscr(f"s_w1{l}", [16, 128, 8, 256], BF16)
        wsc[("w2", l)] = dscr(f"s_w2{l}", [16, 128, 16, 128], BF16)
    spill = {}
    for sname, tt in (("lat", T), ("ctx", TC)):
        spill[("h", sname)] = dscr(f"sp_h_{sname}", [128, 8, tt], BF16)
        spill[("uc", sname)] = dscr(f"sp_uc_{sname}", [128, 8, tt], F32)
        spill[("h1", sname)] = dscr(f"sp_h1_{sname}", [128, 8, tt], F32)
    if debug:
        dbg_z = nc.dram_tensor("dbg_z", [128, 8, NT], BF16, kind="ExternalOutput").ap()
        dbg_pm = nc.dram_tensor("dbg_pm", [128, 4, NT], BF16, kind="ExternalOutput").ap()
        dbg_mg = nc.dram_tensor("dbg_mg", [128, 8, NT], BF16, kind="ExternalOutput").ap()
        dbg_xa = nc.dram_tensor("dbg_xa", [128, 8, NT], F32, kind="ExternalOutput").ap()
    cc_src = [dscr(f"cc_src{i}", [128, 16], F32) for i in range(2 * nl)]
    cc_dst = [dscr(f"cc_dst{i}", [256, 16], F32) for i in range(2 * nl)]

    def sb(name, shape, dt=F32):
        return st.enter_context(nc.sbuf_tensor("sb_" + name, list(shape), dt))

    XL = sb("XL", [128, NCH, T])
    XC = sb("XC", [128, NCH, TC])
    BIG = sb("BIG", [128, NCH * UW], BF16)
    UCX = sb("UCX", [128, NCH, TC + 4], BF16)
    RING_N = 4
    RING = sb("RING", [128, RING_N, 2048], BF16)
    ident = sb("ident", [128, 128])
    onesb = sb("onesb", [128, 128], BF16)
    pml = sb("pml", [128, 4, 128], BF16)
    pmc = sb("pmc", [128, 4, 2, 256], BF16)
    vg = sb("vg", [128, V_GLOBAL])
    vl = sb("vl", [128, V_PER_LAYER])
    gwd = sb("gwd", [128, 2, 8, 128], BF16)
    poolw = sb("poolw", [128, 4, 128], BF16)
    silc = sb("silc", [128, 8, 2], BF16)
    mod = sb("mod", [128, 48, 2])
    dv = sb("dv", [128, 2, 3, 8])
    lruc = sb("lruc", [128, 2, 4, 8])
    tmpv = sb("tmpv", [128, 2, 8])
    CAR = sb("CAR", [128, 3, 8])
    exch = sb("exch", [128, 16])
    exch_in = sb("exch_in", [128, 2, 16])
    selrow = sb("selrow", [128, 16])
    hb = sb("hb", [128, NCH, NT], BF16)
    zb = sb("zb", [128, NCH, NT], BF16)
    mgflat = sb("mg", [128, NCH * NT], BF16)
    sq = sb("sq", [128, 2, NT], BF16)
    rstd = sb("rstd", [128, NT])
    tmp32 = sb("tmp32", [128, 2, NT])
    ucf = sb("ucf", [128, 2, NT])
    ucb = sb("ucb", [128, 2, NT], BF16)
    h1t = sb("h1t", [128, 2, NT])
    thr = sb("thr", [128, 2, NT])
    thi = sb("thi", [128, 2, NT])
    at = sb("at", [128, 2, NT])
    a2t = sb("a2t", [128, 2, NT])
    pT = sb("pT", [128, 4, 512], BF16)
    msb = sb("msb", [128, 4, NT], BF16)
    relu = sb("relu", [128, 2, NT], BF16)
    psum = [st.enter_context(nc.psum_tensor(f"ps{i}", [128, 512], F32)) for i in range(8)]

    mg = mgflat[:, :].rearrange("p (c t) -> p c t", c=NCH)
    pTflat = pT[:, :, :].rearrange("p a b -> p (a b)")
    CD = [mgflat[:, c * 640:(c + 1) * 640].rearrange("p (j e) -> p j e", j=5) for c in range(6)] + \
         [pTflat[:, c * 640:(c + 1) * 640].rearrange("p (j e) -> p j e", j=5) for c in range(2)]
    CDKEYS = [("mg", c) for c in range(8)] + [("pT", s_) for s_ in range(4)]
    U = BIG[:, :].rearrange("p (c t) -> p c t", c=NCH)
    HID = BIG[:, 0:32 * NT].rearrange("p (j t) -> p j t", j=32)
    BIGF = BIG[:, :].bitcast(F32)
    STG = BIGF[:, 0:4096].rearrange("p (a b) -> p a b", a=4)
    YN = BIGF[:, 0:4096].rearrange("p (a b) -> p a b", a=8)
    stage = BIGF[:, 4096:6144].rearrange("p (a b) -> p a b", a=2)
    BIGKEYS = ["bigbar"]

    ps_ctr = [0]

    def next_ps():
        i = ps_ctr[0] % 8
        ps_ctr[0] += 1
        return psum[i], ("ps", i)

    rot = {}

    def nxt(name, n):
        i = rot.get(name, 0)
        rot[name] = i + 1
        return i % n

    def dma(eng, out, in_, reads, writes, sem):
        return P.add(eng, lambda e: e.dma_start(out=out, in_=in_), reads=reads, writes=writes, dma_sem=sem)

    def act(out, in_, func, reads, writes, scale=1.0, bias=0.0):
        return P.add("act", lambda e: e.activation(out=out, in_=in_, func=func, bias=bias, scale=scale),
                     reads=reads, writes=writes)

    def dve(fn, reads, writes):
        return P.add("dve", fn, reads=reads, writes=writes)

    def mm(ps_ap, ps_key, terms, reads):
        def fn(e):
            n = len(terms)
            ins = None
            for i, (l_, r_) in enumerate(terms):
                ins = e.matmul(ps_ap, lhsT=l_, rhs=r_, start=(i == 0), stop=(i == n - 1))
            return ins
        return P.add("pe", fn, reads=reads, writes=[ps_key])

    def big_barrier():
        keys = list(BIGKEYS)
        dve(lambda e: e.memset(selrow[:, 0:1], 0.0), keys, keys + ["selrow"])

    def bigkey(k):
        if k not in BIGKEYS:
            BIGKEYS.append(k)
        return k

    s_const = P.new_dma_sem("const")
    s_const2 = P.new_dma_sem("const2")
    s_const3 = P.new_dma_sem("const3")
    s_const4 = P.new_dma_sem("const4")
    dma("sp", ident[:], ident_d, [], ["ident"], s_const)
    dma("sp", vg[:], vec_g, [], ["vg"], s_const2)
    dma("pool", pml[:], pml_d, [], ["pml"], s_const3)
    dma("pool", pmc[:], pmc_d, [], ["pmc"], s_const4)
    dve(lambda e: e.memset(onesb[:], 1.0 / D), [], ["onesb"])
    dve(lambda e: e.memset(UCX[:], 0.0), [], ["ucx"])
    act(silc[:, :, 0], vg[:, VG_C:VG_C + 8], AF.Silu, ["vg"], ["silc"])
    act(silc[:, :, 1], vg[:, VG_CC:VG_CC + 8], AF.Silu, ["vg", "silc"], ["silc"])

    wgroup = {}

    def prep_weights(l, which="all"):
        todo = []

        def conv(name, src, j0, j1, cw, kcn=8, gsz=4):
            dst = wsc[(name, l)]
            for j in range(j0, j1):
                grp = j // gsz
                sem = f"w_{name}{l}_{grp}"
                if sem not in P.dma_sem_names:
                    P.new_dma_sem(sem)
                if name == "w2":
                    m, hf = j // 2, j % 2
                    srcv = src[:, m * cw:(m + 1) * cw].rearrange("(k p) c -> p k c", p=128)[:, hf * 16:(hf + 1) * 16, :]
                else:
                    srcv = src[:, j * cw:(j + 1) * cw].rearrange("(k p) c -> p k c", p=128)[:, 0:kcn, :]
                wgroup[(name, l, j)] = ("wsc", name, l, grp)
                todo.append(lambda o_=dst[j], i_=srcv, sem=sem, key=("wsc", name, l, grp): P.add(
                    "pool", lambda e: e.dma_start(out=o_, in_=i_), reads=[], writes=[key], dma_sem=sem, waw=False))
        if which in ("all", "early"):
            conv("win", w_in_d[l], 0, 4, 256)
            conv("win", w_in_d[l], 4, 18, 256)
            conv("wlo", w_lo_d[l], 0, 4, 256)
            conv("wpo", w_po_d[l], 0, 2, 512, 4)
            conv("wo", w_o_d[l], 0, 4, 256)
        if which in ("all", "late"):
            conv("w1", w1_d[l], 0, 16, 256)
            conv("w2", w2_d[l], 0, 16, 128, 16)
        return todo

    ring_sems = [P.new_dma_sem(f"ring{i}") for i in range(RING_N)]

    def wblock(name, l, j):
        s_ = nxt("ring", RING_N)
        key = ("ring", s_)
        if name == "w2":
            view = RING[:, s_, :].rearrange("p (k c) -> p k c", k=16)
        elif name == "wpo":
            view = RING[:, s_, :].rearrange("p (k c) -> p k c", k=4)
        else:
            view = RING[:, s_, :].rearrange("p (k c) -> p k c", k=8)
        dma("sp", view, wsc[(name, l)][j], [wgroup[(name, l, j)]], [key], ring_sems[s_])
        return view, key

    def wchunk(name, l, base, m, cache):
        jb = base + m // 2
        if cache.get("jb") != (name, jb):
            cache["jb"] = (name, jb)
            cache["w"] = wblock(name, l, jb)
        wv, wk = cache["w"]
        return wv, wk, slice((m % 2) * 128, (m % 2) * 128 + 128)

    ada_sems = [P.new_dma_sem(f"ada{i}") for i in range(2)]
    gw_sems = [P.new_dma_sem(f"gw{i}") for i in range(2)]

    def load_gw(l, d):
        dma("pool", gwd[:], gw_d[l][:, 2 * d:2 * d + 2, :, :], [], ["gwd"], gw_sems[d])

    def layer_setup(l):
        s_lv = P.new_dma_sem(f"lv{l}")
        s_lv2 = P.new_dma_sem(f"lvp{l}")
        dma("sp", vl[:], vec_l[l], [], ["vl"], s_lv)
        dma("pool", poolw[:], poolw_d[l], [], ["poolw"], s_lv2)
        psm, pk = next_ps()
        for jb in range(24):
            s_ = nxt("ada", 2)
            view = mgflat[:, s_ * 2048:(s_ + 1) * 2048].rearrange("p (k c) -> p k c", k=8)
            akeys = [("mg", c) for c in range(4 * s_, 4 * s_ + 4)]
            dma("pool", view, w_ada_d[l][:, jb * 256:(jb + 1) * 256].rearrange("(k p) c -> p k c", p=128),
                [], akeys, ada_sems[s_])
            for jj in range(2):
                j = jb * 2 + jj

                def fn(e, view=view, jj=jj, j=j):
                    ins = None
                    for kc in range(8):
                        ins = e.matmul(psm[:, 2 * j:2 * j + 2], lhsT=view[:, kc, jj * 128:(jj + 1) * 128],
                                       rhs=silc[:, kc, :], start=(kc == 0), stop=(kc == 7))
                    return ins
                P.add("pe", fn, reads=akeys + ["silc"], writes=[pk] if j == 0 else [(pk, "part")])
        psv = psm[:, 0:96].rearrange("p (j t) -> p j t", t=2)
        for col in range(2):
            dve(lambda e, col=col: e.tensor_tensor(out=mod[:, :, col], in0=psv[:, :, col],
                                                   in1=vl[:, V_BADA:V_BADA + 48], op=ALU.add),
                [pk, (pk, "part"), "vl"], [("mod", col)])
            dve(lambda e, col=col: e.scalar_tensor_tensor(
                out=dv[:, col, 0, :], in0=mod[:, 8:16, col], scalar=1.0, in1=vl[:, V_N1G:V_N1G + 8],
                op0=ALU.add, op1=ALU.mult), [("mod", col), "vl"], [("dv", col)])
            dve(lambda e, col=col: e.scalar_tensor_tensor(
                out=dv[:, col, 1, :], in0=mod[:, 32:40, col], scalar=1.0, in1=vl[:, V_N2G:V_N2G + 8],
                op0=ALU.add, op1=ALU.mult), [("mod", col), "vl", ("dv", col)], [("dv", col)])
            dve(lambda e, col=col: e.tensor_scalar(
                out=dv[:, col, 2, :], in0=mod[:, 16:24, col], scalar1=0.5, scalar2=None, op0=ALU.mult),
                [("mod", col), ("dv", col)], [("dv", col)])
        for d in range(2):
            act(tmpv[:, d, :], vl[:, V_LAM + 8 * d:V_LAM + 8 * d + 8], AF.Exp, ["vl"], [("tmpv", d)], scale=-1.0)
            act(tmpv[:, d, :], tmpv[:, d, :], AF.Ln, [("tmpv", d)], [("tmpv", d)], bias=1.0)
            for k, (srcap, mul) in enumerate((
                    (tmpv[:, d, :], -4.0), (tmpv[:, d, :], -8.0),
                    (vl[:, V_BR + 8 * d:V_BR + 8 * d + 8], 0.5), (vl[:, V_BI + 8 * d:V_BI + 8 * d + 8], 0.5))):
                dve(lambda e, d=d, k=k, srcap=srcap, mul=mul: e.tensor_scalar(
                    out=lruc[:, d, k, :], in0=srcap, scalar1=mul, scalar2=None, op0=ALU.mult),
                    [("tmpv", d), "vl", ("lruc", d)], [("lruc", d)])

    class Seg:
        pass

    lat = Seg()
    lat.name, lat.X, lat.T, lat.col, lat.U = "lat", XL, T, 0, U
    lat.tiles = [(i * NT, NT) for i in range(T // NT)]
    lat.car2 = 1
    ctx = Seg()
    ctx.name, ctx.X, ctx.T, ctx.col, ctx.U = "ctx", XC, TC, 1, UCX
    ctx.tiles = [(0, TC)]
    ctx.car2 = 2

    def xkey(seg, c, ti):
        return ("x", seg.name, c, ti)

    def ukey(seg, c, ti):
        k = ("U", seg.name, c, ti)
        return bigkey(k) if seg is lat else k

    stg_sems = [P.new_dma_sem(f"stg{i}") for i in range(4)]

    def load_seg(seg, src):
        for ti, (t0, n) in enumerate(seg.tiles):
            nsub = n // 128
            for sbi in range(nsub):
                dma("sp", STG[:, sbi, :], src[t0 + sbi * 128:t0 + (sbi + 1) * 128, :], [], [bigkey(("stg", sbi))],
                    stg_sems[sbi])
            for c in range(8):
                ps, pk = next_ps()

                def fn(e, ps=ps, c=c, nsub=nsub):
                    ins = None
                    for sbi in range(nsub):
                        ins = e.transpose(ps[:, sbi * 128:(sbi + 1) * 128], STG[:, sbi, c * 128:(c + 1) * 128], ident[:])
                    return ins
                P.add("pe", fn, reads=[("stg", s) for s in range(nsub)] + ["ident"], writes=[pk])
                if c % 2 == 0:
                    dve(lambda e, ps=ps, c=c, t0=t0, n=n: e.tensor_copy(out=seg.X[:, c, t0:t0 + n], in_=ps[:, 0:n]),
                        [pk], [xkey(seg, c, ti)])
                else:
                    P.add("act", lambda e, ps=ps, c=c, t0=t0, n=n: e.copy(out=seg.X[:, c, t0:t0 + n], in_=ps[:, 0:n]),
                          reads=[pk], writes=[xkey(seg, c, ti)])

    def rms_stats(seg, ti):
        t0, n = seg.tiles[ti]
        ps, pk = next_ps()
        for c in range(8):
            s_ = nxt("sq", 2)
            act(sq[:, s_, 0:n], seg.X[:, c, t0:t0 + n], AF.Square, [xkey(seg, c, ti)], [("sq", s_)])

            def fn(e, s_=s_, c=c, ps=ps, n=n):
                return e.matmul(ps[:, 0:n], lhsT=onesb[:], rhs=sq[:, s_, 0:n], start=(c == 0), stop=(c == 7))
            P.add("pe", fn, reads=[("sq", s_), "onesb"], writes=[pk] if c == 0 else [(pk, "acc")])
        act(rstd[:, 0:n], ps[:, 0:n], AF.Sqrt, [pk, (pk, "acc")], ["rstd"], bias=vg_eps)
        dve(lambda e, n=n: e.reciprocal(out=rstd[:, 0:n], in_=rstd[:, 0:n]), ["rstd"], ["rstd"])

    def norm_mod(seg, ti, which, dst, dkey):
        t0, n = seg.tiles[ti]
        col = seg.col
        rms_stats(seg, ti)
        sh0 = 0 if which == 0 else 24
        for c in range(8):
            s_ = nxt("tmp32", 2)
            dve(lambda e, s_=s_, c=c, t0=t0, n=n: e.tensor_tensor(
                out=tmp32[:, s_, 0:n], in0=seg.X[:, c, t0:t0 + n], in1=rstd[:, 0:n], op=ALU.mult),
                [xkey(seg, c, ti), "rstd"], [("tmp32", s_)])
            act(dst[:, c, 0:n], tmp32[:, s_, 0:n], AF.Identity, [("tmp32", s_), ("dv", col), ("mod", col)],
                [(dkey, c)], scale=dv[:, col, which, c:c + 1], bias=mod[:, sh0 + c, col:col + 1])

    def lru_front(d, c, n, us):
        s_ = nxt("lru", 2)
        ps_r, pkr = next_ps()
        ps_i, pki = next_ps()
        mm(ps_r[:, 0:n], pkr, [(gwd[:, 0, c, :], ucb[:, us, 0:n])], ["gwd", ("ucb", us)])
        mm(ps_i[:, 0:n], pki, [(gwd[:, 1, c, :], ucb[:, us, 0:n])], ["gwd", ("ucb", us)])
        act(thr[:, s_, 0:n], ps_r[:, 0:n], AF.Tanh, [pkr, ("lruc", d)], [("thr", s_)], scale=0.5,
            bias=lruc[:, d, 2, c:c + 1])
        act(thi[:, s_, 0:n], ps_i[:, 0:n], AF.Tanh, [pki, ("lruc", d)], [("thi", s_)], scale=0.5,
            bias=lruc[:, d, 3, c:c + 1])
        act(at[:, s_, 0:n], thr[:, s_, 0:n], AF.Exp, [("thr", s_), ("lruc", d)], [("at", s_)],
            scale=lruc[:, d, 0, c:c + 1], bias=lruc[:, d, 0, c:c + 1])
        act(a2t[:, s_, 0:n], thr[:, s_, 0:n], AF.Exp, [("thr", s_), ("lruc", d)], [("a2t", s_)],
            scale=lruc[:, d, 1, c:c + 1], bias=lruc[:, d, 1, c:c + 1])
        dve(lambda e: e.scalar_tensor_tensor(out=thi[:, s_, 0:n], in0=thi[:, s_, 0:n], scalar=1.0,
                                             in1=ucf[:, us, 0:n], op0=ALU.add, op1=ALU.mult),
            [("thi", s_), ("ucf", us)], [("thi", s_)])
        return s_

    def lru_back(s_, n):
        act(a2t[:, s_, 0:n], a2t[:, s_, 0:n], AF.Sqrt, [("a2t", s_)], [("a2t", s_)], scale=-0.25, bias=vg_q)
        dve(lambda e: e.tensor_tensor(out=thi[:, s_, 0:n], in0=thi[:, s_, 0:n], in1=a2t[:, s_, 0:n], op=ALU.mult),
            [("thi", s_), ("a2t", s_)], [("thi", s_)])

    spill_sems = {}

    def ssem(name):
        if name not in spill_sems:
            spill_sems[name] = P.new_dma_sem(name)
        return spill_sems[name]

    def phase1a(seg, l, need_p2, as_parts=False):
        def tile_part(ti):
            t0, n = seg.tiles[ti]
            which = nxt("p1buf", 2)
            dst, dkey = (hb, "hb") if which == 0 else (zb, "zb")
            norm_mod(seg, ti, 0, dst, dkey)
            hk = [(dkey, c) for c in range(8)]
            if need_p2:
                dma("sp", spill[("h", seg.name)][:, :, t0:t0 + n], dst[:, :, 0:n], hk, [("sp_h", seg.name, ti)],
                    ssem(f"sph{which}"))
            cache = {}
            for m in range(8):
                wv, wk, cs_ = wchunk("win", l, 0, m, cache)
                ps, pk = next_ps()
                mm(ps[:, 0:n], pk, [(wv[:, kc, cs_], dst[:, kc, 0:n]) for kc in range(8)], [wk] + hk)
                dve(lambda e, ps=ps, m=m, t0=t0, n=n: e.tensor_copy(out=seg.U[:, m, 2 + t0:2 + t0 + n], in_=ps[:, 0:n]),
                    [pk], [ukey(seg, m, ti)])
        parts = [(lambda ti=ti: tile_part(ti)) for ti in range(len(seg.tiles))]
        if as_parts:
            return parts
        for p_ in parts:
            p_()

    def build_convd():
        for c in range(8):
            for j in range(5):
                dve(lambda e, c=c, j=j: e.tensor_scalar(
                    out=CD[c][:, j, :], in0=ident[:], scalar1=vl[:, V_CW + c * 5 + j:V_CW + c * 5 + j + 1],
                    scalar2=None, op0=ALU.mult), ["ident", "vl"] + (CDKEYS if (c, j) == (0, 0) else []),
                    CDKEYS if (c, j) in ((0, 0), (7, 4)) else [("cdpart", c, j)])

    def phase1b(seg, l, need_p2, tiles=None, as_parts=False):
        ntile = len(seg.tiles)

        def f1(ti, c):
            t0, n = seg.tiles[ti]
            ps, pk = next_ps()
            rd = [ukey(seg, c, ti)] + CDKEYS
            if ti > 0:
                rd.append(ukey(seg, c, ti - 1))
            if ti + 1 < ntile:
                rd.append(ukey(seg, c, ti + 1))
            if seg is lat:
                if ti == 0:
                    rd.append(bigkey(("Uhalo", "lat")))
                if ti == ntile - 1:
                    rd.append(bigkey(("UhaloR", "lat")))
            else:
                rd.append("ucx")
            mm(ps[:, 0:n], pk, [(CD[c][:, j, :], seg.U[:, c, t0 + j:t0 + j + n]) for j in range(5)], rd)
            us = nxt("ucf", 2)
            dve(lambda e, ps=ps, us=us, c=c, n=n: e.tensor_scalar(
                out=ucf[:, us, 0:n], in0=ps[:, 0:n], scalar1=vl[:, V_CB + c:V_CB + c + 1], scalar2=None, op0=ALU.add),
                [pk, "vl"], [("ucf", us)])
            act(ucb[:, us, 0:n], ucf[:, us, 0:n], AF.Identity, [("ucf", us)], [("ucb", us)])
            if need_p2:
                dma("sp", spill[("uc", seg.name)][:, c, t0:t0 + n], ucf[:, us, 0:n], [("ucf", us)],
                    [("sp_uc", seg.name, c, ti)], ssem(f"spuc{us}"))
            return us

        def back(ti, c, s_):
            t0, n = seg.tiles[ti]
            lru_back(s_, n)
            hs_ = nxt("h1t", 2)
            dve(lambda e, s_=s_, hs_=hs_, c=c, n=n: e.tensor_tensor_scan(
                out=h1t[:, hs_, 0:n], data0=at[:, s_, 0:n], data1=thi[:, s_, 0:n], initial=CAR[:, 0, c:c + 1],
                op0=ALU.mult, op1=ALU.add), [("at", s_), ("thi", s_), ("car", 0, c)], [("h1t", hs_)])
            dve(lambda e, hs_=hs_, c=c, n=n: e.tensor_copy(out=CAR[:, 0, c:c + 1], in_=h1t[:, hs_, n - 1:n]),
                [("h1t", hs_)], [("car", 0, c)])
            if need_p2:
                dma("sp", spill[("h1", seg.name)][:, c, t0:t0 + n], h1t[:, hs_, 0:n], [("h1t", hs_)],
                    [("sp_h1", seg.name, c, ti)], ssem(f"sph1{hs_}"))

        items = [(ti, c) for ti in (range(ntile) if tiles is None else tiles) for c in range(8)]
        def pair(a, b):
            ua = f1(*a)
            ub = f1(*b)
            sa = lru_front(0, a[1], seg.tiles[a[0]][1], ua)
            sb_ = lru_front(0, b[1], seg.tiles[b[0]][1], ub)
            back(*a, sa)
            back(*b, sb_)
        parts = [(lambda a=items[i], b=items[i + 1]: pair(a, b)) for i in range(0, len(items), 2)]
        if as_parts:
            return parts
        for p_ in parts:
            p_()

    cc_ctr = [0]

    def exchange():
        i = cc_ctr[0]
        cc_ctr[0] += 1
        s1 = P.new_dma_sem(f"cca{i}")
        s2 = P.new_dma_sem(f"ccb{i}")
        s3 = P.new_dma_sem(f"ccc{i}")
        dma("pool", cc_src[i], exch[:], ["exch"], [("ccsrc", i)], s1)
        if noexch:
            dma("pool", cc_dst[i][0:128, :], cc_src[i], [("ccsrc", i)], [("ccdst", i)], s2)
            dma("pool", cc_dst[i][128:256, :], cc_src[i], [("ccsrc", i)], [("ccdst", i)], P.new_dma_sem(f"ccd{i}"))
        else:
            P.add("pool", lambda e: e.collective_compute(
                "AllGather", ALU.bypass, replica_groups=[[0, 1], [2, 3], [4, 5], [6, 7]],
                ins=[cc_src[i].opt()], outs=[cc_dst[i].opt()]), reads=[("ccsrc", i)], writes=[("ccdst", i)],
                dma_sem=s2, inc=1)
        dma("pool", exch_in[:], cc_dst[i].rearrange("(r p) k -> p r k", p=128), [("ccdst", i)], ["exch_in"], s3)

    def exchange_finish():
        dve(lambda e: e.tensor_scalar(out=selrow[:], in0=exch_in[:, 0, :], scalar1=vg[:, VG_SEL:VG_SEL + 1],
                                      scalar2=None, op0=ALU.mult), ["exch_in", "vg"], ["selrow"])
        dve(lambda e: e.scalar_tensor_tensor(out=selrow[:], in0=exch_in[:, 1, :], scalar=vg[:, VG_SEL + 1:VG_SEL + 2],
                                             in1=selrow[:], op0=ALU.mult, op1=ALU.add),
            ["exch_in", "vg", "selrow"], ["selrow"])

    def drain(pending, k):
        for _ in range(min(k, len(pending))):
            pending.pop(0)()

    def phase2_tile(seg, l, ti_, pending, pre):
        is_lat = seg is lat
        parts = []
        for ti in (ti_,):
            t0, n = seg.tiles[ti]
            nsub = n // 128
            hbk = [("hb", kc) for kc in range(8)]
            dma("sp", hb[:, :, 0:n], spill[("h", seg.name)][:, :, t0:t0 + n], [("sp_h", seg.name, ti)], hbk, ssem("ldh"))
            cache = {}
            for c in range(8):
                wv, wk, cs_ = wchunk("win", l, 4, c, cache)
                ps, pk = next_ps()
                mm(ps[:, 0:n], pk, [(wv[:, kc, cs_], hb[:, kc, 0:n]) for kc in range(8)], [wk] + hbk)
                act(zb[:, c, 0:n], ps[:, 0:n], AF.Gelu, [pk], [("zb", c)])
            drain(pending, 3)
            while pre:
                pre.pop(0)()

            def f1(c):
                us = nxt("ucf", 2)
                dma("sp", ucf[:, us, 0:n], spill[("uc", seg.name)][:, c, t0:t0 + n], [("sp_uc", seg.name, c, ti)],
                    [("ucf", us)], ssem(f"lduc{us}"))
                hs_ = nxt("h1t", 2)
                dma("sp", h1t[:, hs_, 0:n], spill[("h1", seg.name)][:, c, t0:t0 + n], [("sp_h1", seg.name, c, ti)],
                    [("h1t", hs_)], ssem(f"ldh1{hs_}"))
                act(ucb[:, us, 0:n], ucf[:, us, 0:n], AF.Identity, [("ucf", us)], [("ucb", us)])
                return us, hs_

            def back(c, s_, hs_):
                lru_back(s_, n)
                ck = ("car", seg.car2, c)
                dve(lambda e, s_=s_, c=c: e.tensor_tensor_scan(
                    out=thr[:, s_, 0:n][:, ::-1], data0=at[:, s_, 0:n][:, ::-1], data1=thi[:, s_, 0:n][:, ::-1],
                    initial=CAR[:, seg.car2, c:c + 1], op0=ALU.mult, op1=ALU.add),
                    [("at", s_), ("thi", s_), ck, ("thr", s_)], [("thr", s_)])
                dve(lambda e, s_=s_, c=c: e.tensor_copy(out=CAR[:, seg.car2, c:c + 1], in_=thr[:, s_, 0:1]),
                    [("thr", s_)], [ck])
                dve(lambda e, s_=s_, hs_=hs_: e.tensor_tensor(
                    out=thr[:, s_, 0:n], in0=thr[:, s_, 0:n], in1=h1t[:, hs_, 0:n], op=ALU.add),
                    [("thr", s_), ("h1t", hs_)], [("thr", s_)])
                dve(lambda e, s_=s_, c=c: e.tensor_tensor(
                    out=zb[:, c, 0:n], in0=thr[:, s_, 0:n], in1=zb[:, c, 0:n], op=ALU.mult),
                    [("thr", s_), ("zb", c)], [("zb", c)])

            for c0 in range(0, 8, 2):
                ua, ha = f1(c0)
                ub, hb_ = f1(c0 + 1)
                sa = lru_front(1, c0, n, ua)
                sb_ = lru_front(1, c0 + 1, n, ub)
                back(c0, sa, ha)
                back(c0 + 1, sb_, hb_)
                drain(pending, 3 if c0 < 6 else 16)
            wp = [wblock("win", l, 8), wblock("win", l, 9)]
            for sbi in range(nsub):
                ps, pk = next_ps()

                def fn(e, ps=ps, sbi=sbi, wp=wp):
                    ins = None
                    for hf in range(2):
                        for kc in range(8):
                            ins = e.matmul(ps[:, hf * 256:(hf + 1) * 256], lhsT=hb[:, kc, sbi * 128:(sbi + 1) * 128],
                                           rhs=wp[hf][0][:, kc, :], start=(kc == 0), stop=(kc == 7))
                    return ins
                P.add("pe", fn, reads=[wp[0][1], wp[1][1]] + hbk, writes=[pk])
                if sbi % 2 == 0:
                    dve(lambda e, ps=ps, sbi=sbi: e.tensor_copy(out=pT[:, sbi, :], in_=ps[:, :]), [pk], [("pT", sbi)])
                else:
                    P.add("act", lambda e, ps=ps, sbi=sbi: e.copy(out=pT[:, sbi, :], in_=ps[:, :]),
                          reads=[pk], writes=[("pT", sbi)])
            for g in range(4):
                ps, pk = next_ps()

                def fn(e, ps=ps, g=g, nsub=nsub, lat_=is_lat):
                    ins = None
                    if lat_:
                        for sbi in range(nsub):
                            ins = e.matmul(ps[:, sbi * 128:(sbi + 1) * 128], lhsT=pT[:, sbi, g * 128:(g + 1) * 128],
                                           rhs=pml[:, g, :], start=True, stop=True)
                    else:
                        for b_ in range(2):
                            ins = e.matmul(ps[:, 0:256], lhsT=pT[:, b_, g * 128:(g + 1) * 128], rhs=pmc[:, g, b_, :],
                                           start=(b_ == 0), stop=(b_ == 1))
                    return ins
                P.add("pe", fn, reads=[("pT", s) for s in range(nsub)] + ["pml", "pmc"], writes=[pk])
                dve(lambda e, ps=ps, g=g, n=n: e.tensor_copy(out=msb[:, g, 0:n], in_=ps[:, 0:n]), [pk], [("msb", g)])
                ps2, pk2 = next_ps()
                mm(ps2[:, 0:n], pk2, [(poolw[:, g, :], msb[:, g, 0:n])], ["poolw", ("msb", g)])
                act(msb[:, g, 0:n], ps2[:, 0:n], AF.Identity, [pk2, "vl"], [("msb", g)], scale=vl[:, V_PS + g:V_PS + g + 1])
            c_lo, c_a, c_b = {}, {}, {}
            wpo = None
            for m in range(8):
                if m % 2 == 0:
                    wpo = wblock("wpo", l, m // 4)
                wv, wk, cs_ = wchunk("wlo", l, 0, m, c_lo)
                psA, pkA = next_ps()
                mm(psA[:, 0:n], pkA, [(wv[:, c, cs_], zb[:, c, 0:n]) for c in range(8)], [wk] + [("zb", c) for c in range(8)])
                psB, pkB = next_ps()
                cs4 = slice((m % 4) * 128, (m % 4) * 128 + 128)
                mm(psB[:, 0:n], pkB, [(wpo[0][:, g, cs4], msb[:, g, 0:n]) for g in range(4)],
                   [wpo[1]] + [("msb", g) for g in range(4)])
                wv, wk, cs_ = wchunk("win", l, 10, m, c_a)
                psC, pkC = next_ps()
                mm(psC[:, 0:n], pkC, [(wv[:, kc, cs_], hb[:, kc, 0:n]) for kc in range(8)], [wk] + hbk)
                wv, wk, cs_ = wchunk("win", l, 14, m, c_b)
                psD, pkD = next_ps()
                mm(psD[:, 0:n], pkD, [(wv[:, kc, cs_], hb[:, kc, 0:n]) for kc in range(8)], [wk] + hbk)
                act(tmp32[:, 0, 0:n], psC[:, 0:n], AF.Tanh, [pkC], [("tmp32", 0)], scale=0.5)
                act(tmp32[:, 1, 0:n], psD[:, 0:n], AF.Tanh, [pkD], [("tmp32", 1)], scale=0.5)
                dve(lambda e, psA=psA, n=n: e.scalar_tensor_tensor(
                    out=tmp32[:, 0, 0:n], in0=tmp32[:, 0, 0:n], scalar=1.0, in1=psA[:, 0:n], op0=ALU.add, op1=ALU.mult),
                    [("tmp32", 0), pkA], [("tmp32", 0)])
                dve(lambda e, psB=psB, n=n: e.scalar_tensor_tensor(
                    out=tmp32[:, 1, 0:n], in0=tmp32[:, 1, 0:n], scalar=1.0, in1=psB[:, 0:n], op0=ALU.add, op1=ALU.mult),
                    [("tmp32", 1), pkB], [("tmp32", 1)])
                dve(lambda e, m=m, n=n: e.tensor_tensor(
                    out=mg[:, m, 0:n], in0=tmp32[:, 0, 0:n], in1=tmp32[:, 1, 0:n], op=ALU.add),
                    [("tmp32", 0), ("tmp32", 1)], [("mg", m)])
            if debug and is_lat and ti == len(seg.tiles) - 1 and l == layers[0]:
                dma("sp", dbg_z, zb[:, :, 0:n], [("zb", c) for c in range(8)], ["dbg_z"], P.new_dma_sem("dbgd_a"))
                dma("sp", dbg_pm, msb[:, :, 0:n], [("msb", g) for g in range(4)], ["dbg_pm"], P.new_dma_sem("dbgd_b"))
                dma("sp", dbg_mg, mg[:, :, 0:n], [("mg", c) for c in range(8)], ["dbg_mg"], P.new_dma_sem("dbgd_c"))
            c_o = {}
            for m in range(8):
                wv, wk, cs_ = wchunk("wo", l, 0, m, c_o)
                ps, pk = next_ps()
                mm(ps[:, 0:n], pk, [(wv[:, c, cs_], mg[:, c, 0:n]) for c in range(8)], [wk] + [("mg", c) for c in range(8)])
                dve(lambda e, ps=ps, m=m, t0=t0, n=n: e.scalar_tensor_tensor(
                    out=seg.X[:, m, t0:t0 + n], in0=ps[:, 0:n], scalar=dv[:, seg.col, 2, m:m + 1],
                    in1=seg.X[:, m, t0:t0 + n], op0=ALU.mult, op1=ALU.add),
                    [pk, ("dv", seg.col), xkey(seg, m, ti)], [xkey(seg, m, ti)])
            if debug and is_lat and ti == len(seg.tiles) - 1 and l == layers[0]:
                dsem2 = P.new_dma_sem("dbgdump2")
                dma("sp", dbg_xa, seg.X[:, :, t0:t0 + n], [xkey(seg, m, ti) for m in range(8)], ["dbg_xa"], dsem2)
            norm_mod(seg, ti, 1, mg, "mg")
            zk = [("mg", kc) for kc in range(8)]
            c_1 = {}

            def w1_part(j0, t0=t0, n=n, zk=zk, c_1=c_1):
                for j in range(j0, j0 + 4):
                    wv, wk, cs_ = wchunk("w1", l, 0, j, c_1)
                    ps, pk = next_ps()
                    mm(ps[:, 0:n], pk, [(wv[:, kc, cs_], mg[:, kc, 0:n]) for kc in range(8)], [wk] + zk)
                    r_ = nxt("relu", 2)
                    act(relu[:, r_, 0:n], ps[:, 0:n], AF.Relu, [pk], [("relu", r_)])
                    dve(lambda e, r_=r_, j=j, n=n: e.tensor_tensor(
                        out=HID[:, j, 0:n], in0=relu[:, r_, 0:n], in1=relu[:, r_, 0:n], op=ALU.mult),
                        [("relu", r_)], [bigkey(("hid", j))])

            def w2_part(m, t0=t0, n=n, ti=ti):
                w2a = wblock("w2", l, 2 * m)
                w2b = wblock("w2", l, 2 * m + 1)
                ps, pk = next_ps()
                terms = [(w2a[0][:, j, :], HID[:, j, 0:n]) for j in range(16)] + \
                        [(w2b[0][:, j, :], HID[:, 16 + j, 0:n]) for j in range(16)]
                mm(ps[:, 0:n], pk, terms, [w2a[1], w2b[1]] + [("hid", j) for j in range(32)])
                dve(lambda e, ps=ps, m=m, t0=t0, n=n: e.scalar_tensor_tensor(
                    out=seg.X[:, m, t0:t0 + n], in0=ps[:, 0:n], scalar=mod[:, 40 + m, seg.col:seg.col + 1],
                    in1=seg.X[:, m, t0:t0 + n], op0=ALU.mult, op1=ALU.add),
                    [pk, ("mod", seg.col), xkey(seg, m, ti)], [xkey(seg, m, ti)])

            for j0 in range(0, 32, 4):
                parts.append(lambda j0=j0: w1_part(j0))
            for m in range(8):
                parts.append(lambda m=m: w2_part(m))
        return parts

    out_sems = [P.new_dma_sem(f"outst{i}") for i in range(2)]

    def store_seg(seg, dst, do_norm):
        for ti, (t0, n) in enumerate(seg.tiles):
            nsub = n // 128
            if do_norm:
                rms_stats(seg, ti)
            for c in range(8):
                if do_norm:
                    dve(lambda e, c=c, t0=t0, n=n: e.scalar_tensor_tensor(
                        out=YN[:, c, 0:n], in0=seg.X[:, c, t0:t0 + n], scalar=vg[:, VG_FG + c:VG_FG + c + 1],
                        in1=rstd[:, 0:n], op0=ALU.mult, op1=ALU.mult),
                        [xkey(seg, c, ti), "rstd", "vg"], [bigkey(("yn", c))])
                else:
                    dve(lambda e, c=c, t0=t0, n=n: e.tensor_copy(out=YN[:, c, 0:n], in_=seg.X[:, c, t0:t0 + n]),
                        [xkey(seg, c, ti)], [bigkey(("yn", c))])
            for sbi in range(nsub):
                so = nxt("stage", 2)
                for half in range(2):
                    ps, pk = next_ps()

                    def fn(e, ps=ps, half=half, sbi=sbi):
                        ins = None
                        for cc in range(4):
                            c = half * 4 + cc
                            ins = e.transpose(ps[:, cc * 128:(cc + 1) * 128], YN[:, c, sbi * 128:(sbi + 1) * 128], ident[:])
                        return ins
                    P.add("pe", fn, reads=[("yn", c) for c in range(8)] + ["ident"], writes=[pk])
                    if half == 0:
                        dve(lambda e, ps=ps, so=so: e.tensor_copy(out=stage[:, so, 0:512], in_=ps[:, :]),
                            [pk], [bigkey(("stage", so, 0))])
                    else:
                        P.add("act", lambda e, ps=ps, so=so: e.copy(out=stage[:, so, 512:1024], in_=ps[:, :]),
                              reads=[pk], writes=[bigkey(("stage", so, 1))])
                dma("sp", dst[t0 + sbi * 128:t0 + (sbi + 1) * 128, :], stage[:, so, :], [("stage", so, 0), ("stage", so, 1)],
                    [("outrow", seg.name, ti, sbi)], out_sems[so])

    cst = sb("cst", [128, 2])
    dve(lambda e: e.memset(cst[:, 0:1], EPS), [], ["cst"])
    dve(lambda e: e.memset(cst[:, 1:2], 0.25), ["cst"], ["cst"])
    vg_eps = cst[:, 0:1]
    vg_q = cst[:, 1:2]

    load_seg(ctx, c_in)
    load_seg(lat, x_in)
    for c in range(8):
        for ti in range(len(lat.tiles)):
            ukey(lat, c, ti)
        bigkey(("yn", c))
    for j in range(32):
        bigkey(("hid", j))
    for so in range(2):
        for hf in range(2):
            bigkey(("stage", so, hf))
    bigkey(("Uhalo", "lat"))
    bigkey(("UhaloR", "lat"))
    try:
        chk("load")
        for li, l in enumerate(layers):
            last = (l == DEPTH - 1)
            layer_setup(l)
            chk("setup")
            build_convd()
            load_gw(l, 0)
            if li == 0:
                drain(prep_weights(l, "early"), 999)
            chk("prep")
            big_barrier()
            dve(lambda e: e.memset(U[:, :, 0:2], 0.0), [], [("Uhalo", "lat")])
            dve(lambda e: e.memset(CAR[:], 0.0), [], [("car", i, c) for i in range(3) for c in range(8)])
            phase1a(ctx, l, not last)
            chk("p1a_ctx")
            cparts = phase1b(ctx, l, not last, as_parts=True)
            lparts = phase1a(lat, l, True, as_parts=True)
            while cparts or lparts:
                drain(cparts, 1)
                drain(lparts, 1)
            chk("p1a_lat")
            dve(lambda e: e.tensor_copy(out=exch[:, :].rearrange("p (c k) -> p c k", k=2), in_=U[:, :, T:T + 2]),
                [ukey(lat, c, len(lat.tiles) - 1) for c in range(8)] + ["exch_in"], ["exch"])
            exchange()
            if li == 0:
                drain(prep_weights(l, "late"), 999)
            nlt = len(lat.tiles)
            phase1b(lat, l, True, tiles=list(range(nlt - 1)))
            exchange_finish()
            selv = selrow[:, :].rearrange("p (c k) -> p c k", k=2)
            for k in range(2):
                dve(lambda e, k=k: e.tensor_copy(out=U[:, :, T + 2 + k], in_=selv[:, :, 1 - k]),
                    ["selrow"], [("UhaloR", "lat")])
            chk("halo")
            phase1b(lat, l, True, tiles=[nlt - 1])
            chk("p1_lat")
            dve(lambda e: e.tensor_copy(out=exch[:, 0:8], in_=CAR[:, 0, :]),
                [("car", 0, c) for c in range(8)] + ["exch_in"], ["exch"])
            exchange()

            def apply_carry():
                exchange_finish()
                dve(lambda e: e.tensor_copy(out=CAR[:, 1, :], in_=selrow[:, 0:8]), ["selrow"],
                    [("car", 1, c) for c in range(8)])
            big_barrier()
            load_gw(l, 1)
            nextprep = prep_weights(layers[li + 1]) if li + 1 < len(layers) else []
            chk("carry")
            pending = []
            pre = [apply_carry]
            for sg, ti in [(lat, t_) for t_ in reversed(range(len(lat.tiles)))] + ([] if last else [(ctx, 0)]):
                if not (sg is lat and ti == len(lat.tiles) - 1):
                    drain(nextprep, 20)
                pending = phase2_tile(sg, l, ti, pending, pre)
            drain(pending, 99)
            drain(nextprep, 999)
            chk("layer%d" % l)
    except _Stop:
        pass
    big_barrier()
    if final:
        store_seg(lat, out_d, True)
    else:
        store_seg(lat, out_d, False)
        store_seg(ctx, outc_d, False)
    P.add("sp", lambda e: e.nop(), reads=[k for k in list(P.last_w.keys()) if isinstance(k, tuple) and k[0] == "outrow"],
          writes=["done"])
    P.emit(nc, st)
    st.close()
    return nc


def _fm(v):
    v = np.asarray(v, np.float32)
    return np.ascontiguousarray(v.reshape(-1, 128).T)


def _pool_matrix(L, w):
    t = np.arange(L)
    lo = np.clip(t - w // 2, 0, L)
    hi = np.clip(t + w - w // 2, 0, L)
    M = np.zeros((L, L), np.float32)
    for i in range(L):
        M[i, lo[i]:hi[i]] = np.float32(1.0) / np.float32(hi[i] - lo[i])
        M[i, i] -= 1.0
    return M


def _structural_constants(flip):
    wins = (2, 4, 8, 16)
    pm_lat = np.zeros((128, 4, 128), np.float32)
    pm_ctx = np.zeros((128, 4, 2, 256), np.float32)
    for g, w in enumerate(wins):
        M = _pool_matrix(64, w)
        Mc = _pool_matrix(256, w)
        if flip:
            M = M[::-1, ::-1]
            Mc = Mc[::-1, ::-1]
        blk = np.zeros((128, 128), np.float32)
        blk[:64, :64] = M
        blk[64:, 64:] = M
        pm_lat[:, g, :] = blk.T
        McT = np.ascontiguousarray(Mc.T)
        pm_ctx[:, g, 0, :] = McT[0:128]
        pm_ctx[:, g, 1, :] = McT[128:256]
    return pm_lat, pm_ctx


def _prepare(inputs):
    f = lambda k: np.asarray(inputs[k], np.float32)
    x, c, ctx, c_ctx = f("x"), f("c"), f("ctx"), f("c_ctx")
    shared = {
        "ident": np.eye(128, dtype=np.float32),
        "w_ada": f("w_ada"), "w_in": f("w_in"), "w_lru_out": f("w_lru_out"), "w_pool_out": f("w_pool_out"),
        "w_o": f("w_o"), "mlp_w1": f("mlp_w1"), "mlp_w2": f("mlp_w2"),
        "pool_w": np.ascontiguousarray(f("pool_w").transpose(0, 2, 1, 3)),
    }
    conv_w, conv_b = f("conv_w"), f("conv_b")
    w_r, w_i, b_r, b_i, lam = f("lru_w_r"), f("lru_w_i"), f("lru_b_r"), f("lru_b_i"), f("lru_lambda")
    per_half = []
    for h in range(2):
        dirs = (0, 1) if h == 0 else (1, 0)
        vec_l = np.zeros((DEPTH, 128, V_PER_LAYER), np.float32)
        gw = np.zeros((DEPTH, 128, 4, 8, 128), np.float32)
        for l in range(DEPTH):
            vec_l[l, :, V_N1G:V_N1G + 8] = _fm(inputs["norm1_g"][l])
            vec_l[l, :, V_N2G:V_N2G + 8] = _fm(inputs["norm2_g"][l])
            vec_l[l, :, V_CB:V_CB + 8] = _fm(conv_b[l])
            taps = np.zeros((5, D), np.float32)
            if h == 0:
                taps[0:4] = conv_w[l]
            else:
                taps[1:5] = conv_w[l][::-1]
            for j in range(5):
                vec_l[l, :, V_CW + j:V_CW + 40:5] = _fm(taps[j])
            for dl, d in enumerate(dirs):
                vec_l[l, :, V_BR + 8 * dl:V_BR + 8 * dl + 8] = _fm(b_r[l, d])
                vec_l[l, :, V_BI + 8 * dl:V_BI + 8 * dl + 8] = _fm(b_i[l, d])
                vec_l[l, :, V_LAM + 8 * dl:V_LAM + 8 * dl + 8] = _fm(lam[l, d])
                for gi, wsrc in enumerate((w_r, w_i)):
                    for ch in range(8):
                        for hh in range(2):
                            gw[l, hh * 64:(hh + 1) * 64, 2 * dl + gi, ch, hh * 64:(hh + 1) * 64] = wsrc[l, d, 2 * ch + hh]
            vec_l[l, :, V_PS:V_PS + 4] = _fm(inputs["pool_scale"][l])
            vec_l[l, :, V_BADA:V_BADA + 48] = _fm(inputs["b_ada"][l])
        pm_lat, pm_ctx = _structural_constants(h == 1)
        per_half.append(dict(vec_l=vec_l, gw=gw, pm_lat=pm_lat, pm_ctx=pm_ctx))
    in_maps = []
    for core in range(8):
        b, h = core // 2, core % 2
        xs = x[b, h * T:(h + 1) * T]
        cs = ctx[b]
        if h == 1:
            xs = xs[::-1]
            cs = cs[::-1]
        vec_g = np.zeros((128, V_GLOBAL), np.float32)
        vec_g[:, VG_FG:VG_FG + 8] = _fm(inputs["final_g"])
        vec_g[:, VG_C:VG_C + 8] = _fm(c[b])
        vec_g[:, VG_CC:VG_CC + 8] = _fm(c_ctx)
        vec_g[:, VG_SEL + (1 - h)] = 1.0
        m = dict(shared)
        m.update(per_half[h])
        m["x_in"] = np.ascontiguousarray(xs)
        m["c_in"] = np.ascontiguousarray(cs)
        m["vec_g"] = vec_g
        in_maps.append(m)
    return in_maps


_NC_CACHE = {}


def kernel(**inputs):
    in_maps = _prepare(inputs)
    key = ("full",)
    if key not in _NC_CACHE:
        _NC_CACHE[key] = build_nc((0, 1), True)
    res = run_bass_kernel_spmd(_NC_CACHE[key], in_maps, core_ids=list(range(8)))
    out = np.zeros((NB, SEQ, D), np.float32)
    for core in range(8):
        b, h = core // 2, core % 2
        o = res.results[core]["out"]
        if h == 1:
            o = o[::-1]
        out[b, h * T:(h + 1) * T] = o
    return out
```

```python
import numpy as np
from contextlib import ExitStack
import concourse.bass as bass
import concourse.mybir as mybir
from concourse.bass_utils import run_bass_kernel_spmd

F32 = mybir.dt.float32
BF16 = mybir.dt.bfloat16
AF = mybir.ActivationFunctionType
ALU = mybir.AluOpType

D = 1024
NCH = 8
NB = 4
SEQ = 4096
T = SEQ // 2
TC = 256
DEPTH = 2
D_IN = 4608
D_FF = 4096
EPS = 1e-6
NT = 512
SEM_CAP = 4000
UW = T + 4

V_N1G, V_N2G, V_CB, V_CW, V_BR, V_BI, V_LAM, V_PS, V_BADA = 0, 8, 16, 24, 64, 80, 96, 112, 116
V_PER_LAYER = 164
VG_FG, VG_C, VG_CC, VG_SEL = 0, 8, 16, 24
V_GLOBAL = 26


class Op:
    __slots__ = ("eng", "fn", "deps", "dma_sem", "token", "needed", "inc")

    def __init__(self, eng, fn, dma_sem, inc):
        self.eng, self.fn, self.dma_sem, self.inc = eng, fn, dma_sem, inc
        self.deps = set()
        self.token = None
        self.needed = False


class Prog:
    ENGS = ("pe", "act", "dve", "pool", "sp")

    def __init__(self):
        self.ops = []
        self.last_w = {}
        self.readers = {}
        self.dma_sem_names = []

    def new_dma_sem(self, name):
        self.dma_sem_names.append(name)
        return name

    def add(self, eng, fn, reads=(), writes=(), dma_sem=None, inc=None, waw=True):
        o = Op(eng, fn, dma_sem, inc if inc is not None else (16 if dma_sem else 1))
        for k in reads:
            w = self.last_w.get(k)
            if w is not None:
                o.deps.add(w)
        for k in writes:
            w = self.last_w.get(k)
            if w is not None and waw:
                o.deps.add(w)
            for r in self.readers.get(k, ()):
                o.deps.add(r)
        o.deps.discard(o)
        for k in reads:
            self.readers.setdefault(k, []).append(o)
        for k in writes:
            self.last_w[k] = o
            self.readers[k] = []
        self.ops.append(o)
        return o

    def emit(self, nc, stack):
        for o in self.ops:
            for d in o.deps:
                d.needed = True
        cnt = {e: 0 for e in self.ENGS}
        dcnt = {}
        for o in self.ops:
            if o.dma_sem is not None:
                dcnt[o.dma_sem] = dcnt.get(o.dma_sem, 0) + o.inc
                o.token = (o.dma_sem, dcnt[o.dma_sem])
            elif o.needed:
                cnt[o.eng] += 1
                k = cnt[o.eng]
                o.token = ((o.eng, (k - 1) // SEM_CAP), (k - 1) % SEM_CAP + 1)
        sems = {}
        for e in self.ENGS:
            for ep in range((cnt[e] + SEM_CAP - 1) // SEM_CAP):
                sems[(e, ep)] = stack.enter_context(nc.semaphore(f"s_{e}{ep}"))
        for n in self.dma_sem_names:
            if n in dcnt:
                sems[n] = stack.enter_context(nc.semaphore(f"d_{n}"))
        block = stack.enter_context(nc.Block())
        per_eng = {e: [o for o in self.ops if o.eng == e] for e in self.ENGS}

        def run(eng_name, eng):
            waited = {}
            for o in per_eng[eng_name]:
                need = {}
                for d in o.deps:
                    if d.dma_sem is None and d.eng == "pe" and eng_name == "pe":
                        continue
                    key, val = d.token
                    if need.get(key, 0) < val:
                        need[key] = val
                for key, val in need.items():
                    if waited.get(key, 0) >= val:
                        continue
                    eng.wait_ge(sems[key], val)
                    waited[key] = val
                ins = o.fn(eng)
                if o.token is not None:
                    if o.dma_sem is not None and o.inc == 1:
                        ins.then_inc(sems[o.token[0]])
                    else:
                        ins.then_inc(sems[o.token[0]], o.inc)

        @block.tensor
        def _(e):
            run("pe", e)

        @block.scalar
        def _(e):
            run("act", e)

        @block.vector
        def _(e):
            run("dve", e)

        @block.gpsimd
        def _(e):
            run("pool", e)

        @block.sync
        def _(e):
            run("sp", e)


def build_nc(layers=(0, 1), final=True, stop=None, debug=False, noexch=False):
    nc = bass.Bass("TRN2", target_bir_lowering=False)
    P = Prog()
    st = ExitStack()
    nl = len(layers)

    class _Stop(Exception):
        pass

    def chk(name):
        if stop == name:
            raise _Stop()

    def din(name, shape):
        return nc.dram_tensor(name, list(shape), F32, kind="ExternalInput").ap()

    x_in = din("x_in", [T, D])
    c_in = din("c_in", [TC, D])
    vec_g = din("vec_g", [128, V_GLOBAL])
    vec_l = din("vec_l", [DEPTH, 128, V_PER_LAYER])
    ident_d = din("ident", [128, 128])
    pml_d = din("pm_lat", [128, 4, 128])
    pmc_d = din("pm_ctx", [128, 4, 2, 256])
    gw_d = din("gw", [DEPTH, 128, 4, 8, 128])
    poolw_d = din("pool_w", [DEPTH, 128, 4, 128])
    w_ada_d = din("w_ada", [DEPTH, D, 6 * D])
    w_in_d = din("w_in", [DEPTH, D, D_IN])
    w_lo_d = din("w_lru_out", [DEPTH, D, D])
    w_po_d = din("w_pool_out", [DEPTH, 512, D])
    w_o_d = din("w_o", [DEPTH, D, D])
    w1_d = din("mlp_w1", [DEPTH, D, D_FF])
    w2_d = din("mlp_w2", [DEPTH, D_FF, D])
    out_d = nc.dram_tensor("out", [T, D], F32, kind="ExternalOutput").ap()
    outc_d = None
    if not final:
        outc_d = nc.dram_tensor("out_ctx", [TC, D], F32, kind="ExternalOutput").ap()

    def dscr(name, shape, dt):
        if debug and name.startswith("sp_"):
            return nc.dram_tensor(name, list(shape), dt, kind="ExternalOutput").ap()
        return nc.dram_tensor(name, list(shape), dt).ap()

    wsc = {}
    for l in layers:
        wsc[("win", l)] = dscr(f"s_win{l}", [18, 128, 8, 256], BF16)
        wsc[("wlo", l)] = dscr(f"s_wlo{l}", [4, 128, 8, 256], BF16)
        wsc[("wo", l)] = dscr(f"s_wo{l}", [4, 128, 8, 256], BF16)
        wsc[("wpo", l)] = dscr(f"s_wpo{l}", [2, 128, 4, 512], BF16)
        wsc[("w1", l)] = dscr(f"s_w1{l}", [16, 128, 8, 256], BF16)
        wsc[("w2", l)] = dscr(f"s_w2{l}", [16, 128, 16, 128], BF16)
    spill = {}
    for sname, tt in (("lat", T), ("ctx", TC)):
        spill[("h", sname)] = dscr(f"sp_h_{sname}", [128, 8, tt], BF16)
        spill[("uc", sname)] = dscr(f"sp_uc_{sname}", [128, 8, tt], F32)
        spill[("h1", sname)] = dscr(f"sp_h1_{sname}", [128, 8, tt], F32)
    if debug:
        dbg_z = nc.dram_tensor("dbg_z", [128, 8, NT], BF16, kind="ExternalOutput").ap()
        dbg_pm = nc.dram_tensor("dbg_pm", [128, 4, NT], BF16, kind="ExternalOutput").ap()
        dbg_mg = nc.dram_tensor("dbg_mg", [128, 8, NT], BF16, kind="ExternalOutput").ap()
        dbg_xa = nc.dram_tensor("dbg_xa", [128, 8, NT], F32, kind="ExternalOutput").ap()
    cc_src = [dscr(f"cc_src{i}", [128, 16], F32) for i in range(2 * nl)]
    cc_dst = [dscr(f"cc_dst{i}", [256, 16], F32) for i in range(2 * nl)]

    def sb(name, shape, dt=F32):
        return st.enter_context(nc.sbuf_tensor("sb_" + name, list(shape), dt))

    XL = sb("XL", [128, NCH, T])
    XC = sb("XC", [128, NCH, TC])
    BIG = sb("BIG", [128, NCH * UW], BF16)
    UCX = sb("UCX", [128, NCH, TC + 4], BF16)
    RING_N = 4
    RING = sb("RING", [128, RING_N, 2048], BF16)
    ident = sb("ident", [128, 128])
    onesb = sb("onesb", [128, 128], BF16)
    pml = sb("pml", [128, 4, 128], BF16)
    pmc = sb("pmc", [128, 4, 2, 256], BF16)
    vg = sb("vg", [128, V_GLOBAL])
    vl = sb("vl", [128, V_PER_LAYER])
    gwd = sb("gwd", [128, 2, 8, 128], BF16)
    poolw = sb("poolw", [128, 4, 128], BF16)
    silc = sb("silc", [128, 8, 2], BF16)
    mod = sb("mod", [128, 48, 2])
    dv = sb("dv", [128, 2, 3, 8])
    lruc = sb("lruc", [128, 2, 4, 8])
    tmpv = sb("tmpv", [128, 2, 8])
    CAR = sb("CAR", [128, 3, 8])
    exch = sb("exch", [128, 16])
    exch_in = sb("exch_in", [128, 2, 16])
    selrow = sb("selrow", [128, 16])
    hb = sb("hb", [128, NCH, NT], BF16)
    zb = sb("zb", [128, NCH, NT], BF16)
    mgflat = sb("mg", [128, NCH * NT], BF16)
    sq = sb("sq", [128, 2, NT], BF16)
    rstd = sb("rstd", [128, NT])
    tmp32 = sb("tmp32", [128, 2, NT])
    ucf = sb("ucf", [128, 2, NT])
    ucb = sb("ucb", [128, 2, NT], BF16)
    h1t = sb("h1t", [128, 2, NT])
    thr = sb("thr", [128, 2, NT])
    thi = sb("thi", [128, 2, NT])
    at = sb("at", [128, 2, NT])
    a2t = sb("a2t", [128, 2, NT])
    pT = sb("pT", [128, 4, 512], BF16)
    msb = sb("msb", [128, 4, NT], BF16)
    relu = sb("relu", [128, 2, NT], BF16)
    psum = [st.enter_context(nc.psum_tensor(f"ps{i}", [128, 512], F32)) for i in range(8)]

    mg = mgflat[:, :].rearrange("p (c t) -> p c t", c=NCH)
    pTflat = pT[:, :, :].rearrange("p a b -> p (a b)")
    CD = [mgflat[:, c * 640:(c + 1) * 640].rearrange("p (j e) -> p j e", j=5) for c in range(6)] + \
         [pTflat[:, c * 640:(c + 1) * 640].rearrange("p (j e) -> p j e", j=5) for c in range(2)]
    CDKEYS = [("mg", c) for c in range(8)] + [("pT", s_) for s_ in range(4)]
    U = BIG[:, :].rearrange("p (c t) -> p c t", c=NCH)
    HID = BIG[:, 0:32 * NT].rearrange("p (j t) -> p j t", j=32)
    BIGF = BIG[:, :].bitcast(F32)
    STG = BIGF[:, 0:4096].rearrange("p (a b) -> p a b", a=4)
    YN = BIGF[:, 0:4096].rearrange("p (a b) -> p a b", a=8)
    stage = BIGF[:, 4096:6144].rearrange("p (a b) -> p a b", a=2)
    BIGKEYS = ["bigbar"]

    ps_ctr = [0]

    def next_ps():
        i = ps_ctr[0] % 8
        ps_ctr[0] += 1
        return psum[i], ("ps", i)

    rot = {}

    def nxt(name, n):
        i = rot.get(name, 0)
        rot[name] = i + 1
        return i % n

    def dma(eng, out, in_, reads, writes, sem):
        return P.add(eng, lambda e: e.dma_start(out=out, in_=in_), reads=reads, writes=writes, dma_sem=sem)

    def act(out, in_, func, reads, writes, scale=1.0, bias=0.0):
        return P.add("act", lambda e: e.activation(out=out, in_=in_, func=func, bias=bias, scale=scale),
                     reads=reads, writes=writes)

    def dve(fn, reads, writes):
        return P.add("dve", fn, reads=reads, writes=writes)

    def mm(ps_ap, ps_key, terms, reads):
        def fn(e):
            n = len(terms)
            ins = None
            for i, (l_, r_) in enumerate(terms):
                ins = e.matmul(ps_ap, lhsT=l_, rhs=r_, start=(i == 0), stop=(i == n - 1))
            return ins
        return P.add("pe", fn, reads=reads, writes=[ps_key])

    def big_barrier():
        keys = list(BIGKEYS)
        dve(lambda e: e.memset(selrow[:, 0:1], 0.0), keys, keys + ["selrow"])

    def bigkey(k):
        if k not in BIGKEYS:
            BIGKEYS.append(k)
        return k

    s_const = P.new_dma_sem("const")
    s_const2 = P.new_dma_sem("const2")
    s_const3 = P.new_dma_sem("const3")
    s_const4 = P.new_dma_sem("const4")
    dma("sp", ident[:], ident_d, [], ["ident"], s_const)
    dma("sp", vg[:], vec_g, [], ["vg"], s_const2)
    dma("pool", pml[:], pml_d, [], ["pml"], s_const3)
    dma("pool", pmc[:], pmc_d, [], ["pmc"], s_const4)
    dve(lambda e: e.memset(onesb[:], 1.0 / D), [], ["onesb"])
    dve(lambda e: e.memset(UCX[:], 0.0), [], ["ucx"])
    act(silc[:, :, 0], vg[:, VG_C:VG_C + 8], AF.Silu, ["vg"], ["silc"])
    act(silc[:, :, 1], vg[:, VG_CC:VG_CC + 8], AF.Silu, ["vg", "silc"], ["silc"])

    wgroup = {}

    def prep_weights(l, which="all"):
        todo = []

        def conv(name, src, j0, j1, cw, kcn=8, gsz=4):
            dst = wsc[(name, l)]
            for j in range(j0, j1):
                grp = j // gsz
                sem = f"w_{name}{l}_{grp}"
                if sem not in P.dma_sem_names:
                    P.new_dma_sem(sem)
                if name == "w2":
                    m, hf = j // 2, j % 2
                    srcv = src[:, m * cw:(m + 1) * cw].rearrange("(k p) c -> p k c", p=128)[:, hf * 16:(hf + 1) * 16, :]
                else:
                    srcv = src[:, j * cw:(j + 1) * cw].rearrange("(k p) c -> p k c", p=128)[:, 0:kcn, :]
                wgroup[(name, l, j)] = ("wsc", name, l, grp)
                todo.append(lambda o_=dst[j], i_=srcv, sem=sem, key=("wsc", name, l, grp): P.add(
                    "pool", lambda e: e.dma_start(out=o_, in_=i_), reads=[], writes=[key], dma_sem=sem, waw=False))
        if which in ("all", "early"):
            conv("win", w_in_d[l], 0, 4, 256)
            conv("win", w_in_d[l], 4, 18, 256)
            conv("wlo", w_lo_d[l], 0, 4, 256)
            conv("wpo", w_po_d[l], 0, 2, 512, 4)
            conv("wo", w_o_d[l], 0, 4, 256)
        if which in ("all", "late"):
            conv("w1", w1_d[l], 0, 16, 256)
            conv("w2", w2_d[l], 0, 16, 128, 16)
        return todo

    ring_sems = [P.new_dma_sem(f"ring{i}") for i in range(RING_N)]

    def wblock(name, l, j):
        s_ = nxt("ring", RING_N)
        key = ("ring", s_)
        if name == "w2":
            view = RING[:, s_, :].rearrange("p (k c) -> p k c", k=16)
        elif name == "wpo":
            view = RING[:, s_, :].rearrange("p (k c) -> p k c", k=4)
        else:
            view = RING[:, s_, :].rearrange("p (k c) -> p k c", k=8)
        dma("sp", view, wsc[(name, l)][j], [wgroup[(name, l, j)]], [key], ring_sems[s_])
        return view, key

    def wchunk(name, l, base, m, cache):
        jb = base + m // 2
        if cache.get("jb") != (name, jb):
            cache["jb"] = (name, jb)
            cache["w"] = wblock(name, l, jb)
        wv, wk = cache["w"]
        return wv, wk, slice((m % 2) * 128, (m % 2) * 128 + 128)

    ada_sems = [P.new_dma_sem(f"ada{i}") for i in range(2)]
    gw_sems = [P.new_dma_sem(f"gw{i}") for i in range(2)]

    def load_gw(l, d):
        dma("pool", gwd[:], gw_d[l][:, 2 * d:2 * d + 2, :, :], [], ["gwd"], gw_sems[d])

    def layer_setup(l):
        s_lv = P.new_dma_sem(f"lv{l}")
        s_lv2 = P.new_dma_sem(f"lvp{l}")
        dma("sp", vl[:], vec_l[l], [], ["vl"], s_lv)
        dma("pool", poolw[:], poolw_d[l], [], ["poolw"], s_lv2)
        psm, pk = next_ps()
        for jb in range(24):
            s_ = nxt("ada", 2)
            view = mgflat[:, s_ * 2048:(s_ + 1) * 2048].rearrange("p (k c) -> p k c", k=8)
            akeys = [("mg", c) for c in range(4 * s_, 4 * s_ + 4)]
            dma("pool", view, w_ada_d[l][:, jb * 256:(jb + 1) * 256].rearrange("(k p) c -> p k c", p=128),
                [], akeys, ada_sems[s_])
            for jj in range(2):
                j = jb * 2 + jj

                def fn(e, view=view, jj=jj, j=j):
                    ins = None
                    for kc in range(8):
                        ins = e.matmul(psm[:, 2 * j:2 * j + 2], lhsT=view[:, kc, jj * 128:(jj + 1) * 128],
                                       rhs=silc[:, kc, :], start=(kc == 0), stop=(kc == 7))
                    return ins
                P.add("pe", fn, reads=akeys + ["silc"], writes=[pk] if j == 0 else [(pk, "part")])
        psv = psm[:, 0:96].rearrange("p (j t) -> p j t", t=2)
        for col in range(2):
            dve(lambda e, col=col: e.tensor_tensor(out=mod[:, :, col], in0=psv[:, :, col],
                                                   in1=vl[:, V_BADA:V_BADA + 48], op=ALU.add),
                [pk, (pk, "part"), "vl"], [("mod", col)])
            dve(lambda e, col=col: e.scalar_tensor_tensor(
                out=dv[:, col, 0, :], in0=mod[:, 8:16, col], scalar=1.0, in1=vl[:, V_N1G:V_N1G + 8],
                op0=ALU.add, op1=ALU.mult), [("mod", col), "vl"], [("dv", col)])
            dve(lambda e, col=col: e.scalar_tensor_tensor(
                out=dv[:, col, 1, :], in0=mod[:, 32:40, col], scalar=1.0, in1=vl[:, V_N2G:V_N2G + 8],
                op0=ALU.add, op1=ALU.mult), [("mod", col), "vl", ("dv", col)], [("dv", col)])
            dve(lambda e, col=col: e.tensor_scalar(
                out=dv[:, col, 2, :], in0=mod[:, 16:24, col], scalar1=0.5, scalar2=None, op0=ALU.mult),
                [("mod", col), ("dv", col)], [("dv", col)])
        for d in range(2):
            act(tmpv[:, d, :], vl[:, V_LAM + 8 * d:V_LAM + 8 * d + 8], AF.Exp, ["vl"], [("tmpv", d)], scale=-1.0)
            act(tmpv[:, d, :], tmpv[:, d, :], AF.Ln, [("tmpv", d)], [("tmpv", d)], bias=1.0)
            for k, (srcap, mul) in enumerate((
                    (tmpv[:, d, :], -4.0), (tmpv[:, d, :], -8.0),
                    (vl[:, V_BR + 8 * d:V_BR + 8 * d + 8], 0.5), (vl[:, V_BI + 8 * d:V_BI + 8 * d + 8], 0.5))):
                dve(lambda e, d=d, k=k, srcap=srcap, mul=mul: e.tensor_scalar(
                    out=lruc[:, d, k, :], in0=srcap, scalar1=mul, scalar2=None, op0=ALU.mult),
                    [("tmpv", d), "vl", ("lruc", d)], [("lruc", d)])

    class Seg:
        pass

    lat = Seg()
    lat.name, lat.X, lat.T, lat.col, lat.U = "lat", XL, T, 0, U
    lat.tiles = [(i * NT, NT) for i in range(T // NT)]
    lat.car2 = 1
    ctx = Seg()
    ctx.name, ctx.X, ctx.T, ctx.col, ctx.U = "ctx", XC, TC, 1, UCX
    ctx.tiles = [(0, TC)]
    ctx.car2 = 2

    def xkey(seg, c, ti):
        return ("x", seg.name, c, ti)

    def ukey(seg, c, ti):
        k = ("U", seg.name, c, ti)
        return bigkey(k) if seg is lat else k

    stg_sems = [P.new_dma_sem(f"stg{i}") for i in range(4)]

    def load_seg(seg, src):
        for ti, (t0, n) in enumerate(seg.tiles):
            nsub = n // 128
            for sbi in range(nsub):
                dma("sp", STG[:, sbi, :], src[t0 + sbi * 128:t0 + (sbi + 1) * 128, :], [], [bigkey(("stg", sbi))],
                    stg_sems[sbi])
            for c in range(8):
                ps, pk = next_ps()

                def fn(e, ps=ps, c=c, nsub=nsub):
                    ins = None
                    for sbi in range(nsub):
                        ins = e.transpose(ps[:, sbi * 128:(sbi + 1) * 128], STG[:, sbi, c * 128:(c + 1) * 128], ident[:])
                    return ins
                P.add("pe", fn, reads=[("stg", s) for s in range(nsub)] + ["ident"], writes=[pk])
                if c % 2 == 0:
                    dve(lambda e, ps=ps, c=c, t0=t0, n=n: e.tensor_copy(out=seg.X[:, c, t0:t0 + n], in_=ps[:, 0:n]),
                        [pk], [xkey(seg, c, ti)])
                else:
                    P.add("act", lambda e, ps=ps, c=c, t0=t0, n=n: e.copy(out=seg.X[:, c, t0:t0 + n], in_=ps[:, 0:n]),
                          reads=[pk], writes=[xkey(seg, c, ti)])

    def rms_stats(seg, ti):
        t0, n = seg.tiles[ti]
        ps, pk = next_ps()
        for c in range(8):
            s_ = nxt("sq", 2)
            act(sq[:, s_, 0:n], seg.X[:, c, t0:t0 + n], AF.Square, [xkey(seg, c, ti)], [("sq", s_)])

            def fn(e, s_=s_, c=c, ps=ps, n=n):
                return e.matmul(ps[:, 0:n], lhsT=onesb[:], rhs=sq[:, s_, 0:n], start=(c == 0), stop=(c == 7))
            P.add("pe", fn, reads=[("sq", s_), "onesb"], writes=[pk] if c == 0 else [(pk, "acc")])
        act(rstd[:, 0:n], ps[:, 0:n], AF.Sqrt, [pk, (pk, "acc")], ["rstd"], bias=vg_eps)
        dve(lambda e, n=n: e.reciprocal(out=rstd[:, 0:n], in_=rstd[:, 0:n]), ["rstd"], ["rstd"])

    def norm_mod(seg, ti, which, dst, dkey):
        t0, n = seg.tiles[ti]
        col = seg.col
        rms_stats(seg, ti)
        sh0 = 0 if which == 0 else 24
        for c in range(8):
            s_ = nxt("tmp32", 2)
            dve(lambda e, s_=s_, c=c, t0=t0, n=n: e.tensor_tensor(
                out=tmp32[:, s_, 0:n], in0=seg.X[:, c, t0:t0 + n], in1=rstd[:, 0:n], op=ALU.mult),
                [xkey(seg, c, ti), "rstd"], [("tmp32", s_)])
            act(dst[:, c, 0:n], tmp32[:, s_, 0:n], AF.Identity, [("tmp32", s_), ("dv", col), ("mod", col)],
                [(dkey, c)], scale=dv[:, col, which, c:c + 1], bias=mod[:, sh0 + c, col:col + 1])

    def lru_front(d, c, n, us):
        s_ = nxt("lru", 2)
        ps_r, pkr = next_ps()
        ps_i, pki = next_ps()
        mm(ps_r[:, 0:n], pkr, [(gwd[:, 0, c, :], ucb[:, us, 0:n])], ["gwd", ("ucb", us)])
        mm(ps_i[:, 0:n], pki, [(gwd[:, 1, c, :], ucb[:, us, 0:n])], ["gwd", ("ucb", us)])
        act(thr[:, s_, 0:n], ps_r[:, 0:n], AF.Tanh, [pkr, ("lruc", d)], [("thr", s_)], scale=0.5,
            bias=lruc[:, d, 2, c:c + 1])
        act(thi[:, s_, 0:n], ps_i[:, 0:n], AF.Tanh, [pki, ("lruc", d)], [("thi", s_)], scale=0.5,
            bias=lruc[:, d, 3, c:c + 1])
        act(at[:, s_, 0:n], thr[:, s_, 0:n], AF.Exp, [("thr", s_), ("lruc", d)], [("at", s_)],
            scale=lruc[:, d, 0, c:c + 1], bias=lruc[:, d, 0, c:c + 1])
        dve(lambda e: e.tensor_tensor(out=a2t[:, s_, 0:n], in0=at[:, s_, 0:n], in1=at[:, s_, 0:n], op=ALU.mult),
            [("at", s_)], [("a2t", s_)])
        dve(lambda e: e.scalar_tensor_tensor(out=thi[:, s_, 0:n], in0=thi[:, s_, 0:n], scalar=1.0,
                                             in1=ucf[:, us, 0:n], op0=ALU.add, op1=ALU.mult),
            [("thi", s_), ("ucf", us)], [("thi", s_)])
        return s_

    def lru_back(s_, n):
        act(a2t[:, s_, 0:n], a2t[:, s_, 0:n], AF.Sqrt, [("a2t", s_)], [("a2t", s_)], scale=-0.25, bias=vg_q)
        dve(lambda e: e.tensor_tensor(out=thi[:, s_, 0:n], in0=thi[:, s_, 0:n], in1=a2t[:, s_, 0:n], op=ALU.mult),
            [("thi", s_), ("a2t", s_)], [("thi", s_)])

    spill_sems = {}

    def ssem(name):
        if name not in spill_sems:
            spill_sems[name] = P.new_dma_sem(name)
        return spill_sems[name]

    def phase1a(seg, l, need_p2, as_parts=False):
        def tile_part(ti):
            t0, n = seg.tiles[ti]
            which = nxt("p1buf", 2)
            dst, dkey = (hb, "hb") if which == 0 else (zb, "zb")
            norm_mod(seg, ti, 0, dst, dkey)
            hk = [(dkey, c) for c in range(8)]
            if need_p2:
                dma("sp", spill[("h", seg.name)][:, :, t0:t0 + n], dst[:, :, 0:n], hk, [("sp_h", seg.name, ti)],
                    ssem(f"sph{which}"))
            cache = {}
            for m in range(8):
                wv, wk, cs_ = wchunk("win", l, 0, m, cache)
                ps, pk = next_ps()
                mm(ps[:, 0:n], pk, [(wv[:, kc, cs_], dst[:, kc, 0:n]) for kc in range(8)], [wk] + hk)
                dve(lambda e, ps=ps, m=m, t0=t0, n=n: e.tensor_copy(out=seg.U[:, m, 2 + t0:2 + t0 + n], in_=ps[:, 0:n]),
                    [pk], [ukey(seg, m, ti)])
        parts = [(lambda ti=ti: tile_part(ti)) for ti in range(len(seg.tiles))]
        if as_parts:
            return parts
        for p_ in parts:
            p_()

    def build_convd():
        for c in range(8):
            for j in range(5):
                dve(lambda e, c=c, j=j: e.tensor_scalar(
                    out=CD[c][:, j, :], in0=ident[:], scalar1=vl[:, V_CW + c * 5 + j:V_CW + c * 5 + j + 1],
                    scalar2=None, op0=ALU.mult), ["ident", "vl"] + (CDKEYS if (c, j) == (0, 0) else []),
                    CDKEYS if (c, j) in ((0, 0), (7, 4)) else [("cdpart", c, j)])

    def phase1b(seg, l, need_p2, tiles=None, as_parts=False):
        ntile = len(seg.tiles)

        def f1(ti, c):
            t0, n = seg.tiles[ti]
            ps, pk = next_ps()
            rd = [ukey(seg, c, ti)] + CDKEYS
            if ti > 0:
                rd.append(ukey(seg, c, ti - 1))
            if ti + 1 < ntile:
                rd.append(ukey(seg, c, ti + 1))
            if seg is lat:
                if ti == 0:
                    rd.append(bigkey(("Uhalo", "lat")))
                if ti == ntile - 1:
                    rd.append(bigkey(("UhaloR", "lat")))
            else:
                rd.append("ucx")
            mm(ps[:, 0:n], pk, [(CD[c][:, j, :], seg.U[:, c, t0 + j:t0 + j + n]) for j in range(5)], rd)
            us = nxt("ucf", 2)
            dve(lambda e, ps=ps, us=us, c=c, n=n: e.tensor_scalar(
                out=ucf[:, us, 0:n], in0=ps[:, 0:n], scalar1=vl[:, V_CB + c:V_CB + c + 1], scalar2=None, op0=ALU.add),
                [pk, "vl"], [("ucf", us)])
            dve(lambda e, us=us, n=n: e.tensor_copy(out=ucb[:, us, 0:n], in_=ucf[:, us, 0:n]), [("ucf", us)], [("ucb", us)])
            if need_p2:
                dma("sp", spill[("uc", seg.name)][:, c, t0:t0 + n], ucf[:, us, 0:n], [("ucf", us)],
                    [("sp_uc", seg.name, c, ti)], ssem(f"spuc{us}"))
            return us

        def back(ti, c, s_):
            t0, n = seg.tiles[ti]
            lru_back(s_, n)
            hs_ = nxt("h1t", 2)
            dve(lambda e, s_=s_, hs_=hs_, c=c, n=n: e.tensor_tensor_scan(
                out=h1t[:, hs_, 0:n], data0=at[:, s_, 0:n], data1=thi[:, s_, 0:n], initial=CAR[:, 0, c:c + 1],
                op0=ALU.mult, op1=ALU.add), [("at", s_), ("thi", s_), ("car", 0, c)], [("h1t", hs_)])
            dve(lambda e, hs_=hs_, c=c, n=n: e.tensor_copy(out=CAR[:, 0, c:c + 1], in_=h1t[:, hs_, n - 1:n]),
                [("h1t", hs_)], [("car", 0, c)])
            if need_p2:
                dma("sp", spill[("h1", seg.name)][:, c, t0:t0 + n], h1t[:, hs_, 0:n], [("h1t", hs_)],
                    [("sp_h1", seg.name, c, ti)], ssem(f"sph1{hs_}"))

        items = [(ti, c) for ti in (range(ntile) if tiles is None else tiles) for c in range(8)]
        def pair(a, b):
            ua = f1(*a)
            ub = f1(*b)
            sa = lru_front(0, a[1], seg.tiles[a[0]][1], ua)
            sb_ = lru_front(0, b[1], seg.tiles[b[0]][1], ub)
            back(*a, sa)
            back(*b, sb_)
        parts = [(lambda a=items[i], b=items[i + 1]: pair(a, b)) for i in range(0, len(items), 2)]
        if as_parts:
            return parts
        for p_ in parts:
            p_()

    cc_ctr = [0]

    def exchange():
        i = cc_ctr[0]
        cc_ctr[0] += 1
        s1 = P.new_dma_sem(f"cca{i}")
        s2 = P.new_dma_sem(f"ccb{i}")
        s3 = P.new_dma_sem(f"ccc{i}")
        dma("pool", cc_src[i], exch[:], ["exch"], [("ccsrc", i)], s1)
        if noexch:
            dma("pool", cc_dst[i][0:128, :], cc_src[i], [("ccsrc", i)], [("ccdst", i)], s2)
            dma("pool", cc_dst[i][128:256, :], cc_src[i], [("ccsrc", i)], [("ccdst", i)], P.new_dma_sem(f"ccd{i}"))
        else:
            P.add("pool", lambda e: e.collective_compute(
                "AllGather", ALU.bypass, replica_groups=[[0, 1], [2, 3], [4, 5], [6, 7]],
                ins=[cc_src[i].opt()], outs=[cc_dst[i].opt()]), reads=[("ccsrc", i)], writes=[("ccdst", i)],
                dma_sem=s2, inc=1)
        dma("pool", exch_in[:], cc_dst[i].rearrange("(r p) k -> p r k", p=128), [("ccdst", i)], ["exch_in"], s3)

    def exchange_finish():
        dve(lambda e: e.tensor_scalar(out=selrow[:], in0=exch_in[:, 0, :], scalar1=vg[:, VG_SEL:VG_SEL + 1],
                                      scalar2=None, op0=ALU.mult), ["exch_in", "vg"], ["selrow"])
        dve(lambda e: e.scalar_tensor_tensor(out=selrow[:], in0=exch_in[:, 1, :], scalar=vg[:, VG_SEL + 1:VG_SEL + 2],
                                             in1=selrow[:], op0=ALU.mult, op1=ALU.add),
            ["exch_in", "vg", "selrow"], ["selrow"])

    def drain(pending, k):
        for _ in range(min(k, len(pending))):
            pending.pop(0)()

    def phase2_tile(seg, l, ti_, pending, pre):
        is_lat = seg is lat
        parts = []
        for ti in (ti_,):
            t0, n = seg.tiles[ti]
            nsub = n // 128
            hbk = [("hb", kc) for kc in range(8)]
            dma("sp", hb[:, :, 0:n], spill[("h", seg.name)][:, :, t0:t0 + n], [("sp_h", seg.name, ti)], hbk, ssem("ldh"))
            cache = {}
            for c in range(8):
                wv, wk, cs_ = wchunk("win", l, 4, c, cache)
                ps, pk = next_ps()
                mm(ps[:, 0:n], pk, [(wv[:, kc, cs_], hb[:, kc, 0:n]) for kc in range(8)], [wk] + hbk)
                act(zb[:, c, 0:n], ps[:, 0:n], AF.Gelu, [pk], [("zb", c)])
            drain(pending, 3)
            while pre:
                pre.pop(0)()

            def f1(c):
                us = nxt("ucf", 2)
                dma("sp", ucf[:, us, 0:n], spill[("uc", seg.name)][:, c, t0:t0 + n], [("sp_uc", seg.name, c, ti)],
                    [("ucf", us)], ssem(f"lduc{us}"))
                hs_ = nxt("h1t", 2)
                dma("sp", h1t[:, hs_, 0:n], spill[("h1", seg.name)][:, c, t0:t0 + n], [("sp_h1", seg.name, c, ti)],
                    [("h1t", hs_)], ssem(f"ldh1{hs_}"))
                dve(lambda e, us=us: e.tensor_copy(out=ucb[:, us, 0:n], in_=ucf[:, us, 0:n]), [("ucf", us)], [("ucb", us)])
                return us, hs_

            def back(c, s_, hs_):
                lru_back(s_, n)
                ck = ("car", seg.car2, c)
                dve(lambda e, s_=s_, c=c: e.tensor_tensor_scan(
                    out=thr[:, s_, 0:n][:, ::-1], data0=at[:, s_, 0:n][:, ::-1], data1=thi[:, s_, 0:n][:, ::-1],
                    initial=CAR[:, seg.car2, c:c + 1], op0=ALU.mult, op1=ALU.add),
                    [("at", s_), ("thi", s_), ck, ("thr", s_)], [("thr", s_)])
                dve(lambda e, s_=s_, c=c: e.tensor_copy(out=CAR[:, seg.car2, c:c + 1], in_=thr[:, s_, 0:1]),
                    [("thr", s_)], [ck])
                dve(lambda e, s_=s_, hs_=hs_: e.tensor_tensor(
                    out=thr[:, s_, 0:n], in0=thr[:, s_, 0:n], in1=h1t[:, hs_, 0:n], op=ALU.add),
                    [("thr", s_), ("h1t", hs_)], [("thr", s_)])
                dve(lambda e, s_=s_, c=c: e.tensor_tensor(
                    out=zb[:, c, 0:n], in0=thr[:, s_, 0:n], in1=zb[:, c, 0:n], op=ALU.mult),
                    [("thr", s_), ("zb", c)], [("zb", c)])

            for c0 in range(0, 8, 2):
                ua, ha = f1(c0)
                ub, hb_ = f1(c0 + 1)
                sa = lru_front(1, c0, n, ua)
                sb_ = lru_front(1, c0 + 1, n, ub)
                back(c0, sa, ha)
                back(c0 + 1, sb_, hb_)
                drain(pending, 3 if c0 < 6 else 16)
            wp = [wblock("win", l, 8), wblock("win", l, 9)]
            for sbi in range(nsub):
                ps, pk = next_ps()

                def fn(e, ps=ps, sbi=sbi, wp=wp):
                    ins = None
                    for hf in range(2):
                        for kc in range(8):
                            ins = e.matmul(ps[:, hf * 256:(hf + 1) * 256], lhsT=hb[:, kc, sbi * 128:(sbi + 1) * 128],
                                           rhs=wp[hf][0][:, kc, :], start=(kc == 0), stop=(kc == 7))
                    return ins
                P.add("pe", fn, reads=[wp[0][1], wp[1][1]] + hbk, writes=[pk])
                if sbi % 2 == 0:
                    dve(lambda e, ps=ps, sbi=sbi: e.tensor_copy(out=pT[:, sbi, :], in_=ps[:, :]), [pk], [("pT", sbi)])
                else:
                    P.add("act", lambda e, ps=ps, sbi=sbi: e.copy(out=pT[:, sbi, :], in_=ps[:, :]),
                          reads=[pk], writes=[("pT", sbi)])
            for g in range(4):
                ps, pk = next_ps()

                def fn(e, ps=ps, g=g, nsub=nsub, lat_=is_lat):
                    ins = None
                    if lat_:
                        for sbi in range(nsub):
                            ins = e.matmul(ps[:, sbi * 128:(sbi + 1) * 128], lhsT=pT[:, sbi, g * 128:(g + 1) * 128],
                                           rhs=pml[:, g, :], start=True, stop=True)
                    else:
                        for b_ in range(2):
                            ins = e.matmul(ps[:, 0:256], lhsT=pT[:, b_, g * 128:(g + 1) * 128], rhs=pmc[:, g, b_, :],
                                           start=(b_ == 0), stop=(b_ == 1))
                    return ins
                P.add("pe", fn, reads=[("pT", s) for s in range(nsub)] + ["pml", "pmc"], writes=[pk])
                dve(lambda e, ps=ps, g=g, n=n: e.tensor_copy(out=msb[:, g, 0:n], in_=ps[:, 0:n]), [pk], [("msb", g)])
                ps2, pk2 = next_ps()
                mm(ps2[:, 0:n], pk2, [(poolw[:, g, :], msb[:, g, 0:n])], ["poolw", ("msb", g)])
                act(msb[:, g, 0:n], ps2[:, 0:n], AF.Identity, [pk2, "vl"], [("msb", g)], scale=vl[:, V_PS + g:V_PS + g + 1])
            c_lo, c_a, c_b = {}, {}, {}
            wpo = None
            for m in range(8):
                if m % 2 == 0:
                    wpo = wblock("wpo", l, m // 4)
                wv, wk, cs_ = wchunk("wlo", l, 0, m, c_lo)
                psA, pkA = next_ps()
                mm(psA[:, 0:n], pkA, [(wv[:, c, cs_], zb[:, c, 0:n]) for c in range(8)], [wk] + [("zb", c) for c in range(8)])
                psB, pkB = next_ps()
                cs4 = slice((m % 4) * 128, (m % 4) * 128 + 128)
                mm(psB[:, 0:n], pkB, [(wpo[0][:, g, cs4], msb[:, g, 0:n]) for g in range(4)],
                   [wpo[1]] + [("msb", g) for g in range(4)])
                wv, wk, cs_ = wchunk("win", l, 10, m, c_a)
                psC, pkC = next_ps()
                mm(psC[:, 0:n], pkC, [(wv[:, kc, cs_], hb[:, kc, 0:n]) for kc in range(8)], [wk] + hbk)
                wv, wk, cs_ = wchunk("win", l, 14, m, c_b)
                psD, pkD = next_ps()
                mm(psD[:, 0:n], pkD, [(wv[:, kc, cs_], hb[:, kc, 0:n]) for kc in range(8)], [wk] + hbk)
                act(tmp32[:, 0, 0:n], psC[:, 0:n], AF.Tanh, [pkC], [("tmp32", 0)], scale=0.5)
                act(tmp32[:, 1, 0:n], psD[:, 0:n], AF.Tanh, [pkD], [("tmp32", 1)], scale=0.5)
                dve(lambda e, psA=psA, n=n: e.scalar_tensor_tensor(
                    out=tmp32[:, 0, 0:n], in0=tmp32[:, 0, 0:n], scalar=1.0, in1=psA[:, 0:n], op0=ALU.add, op1=ALU.mult),
                    [("tmp32", 0), pkA], [("tmp32", 0)])
                dve(lambda e, psB=psB, n=n: e.scalar_tensor_tensor(
                    out=tmp32[:, 1, 0:n], in0=tmp32[:, 1, 0:n], scalar=1.0, in1=psB[:, 0:n], op0=ALU.add, op1=ALU.mult),
                    [("tmp32", 1), pkB], [("tmp32", 1)])
                dve(lambda e, m=m, n=n: e.tensor_tensor(
                    out=mg[:, m, 0:n], in0=tmp32[:, 0, 0:n], in1=tmp32[:, 1, 0:n], op=ALU.add),
                    [("tmp32", 0), ("tmp32", 1)], [("mg", m)])
            if debug and is_lat and ti == len(seg.tiles) - 1 and l == layers[0]:
                dma("sp", dbg_z, zb[:, :, 0:n], [("zb", c) for c in range(8)], ["dbg_z"], P.new_dma_sem("dbgd_a"))
                dma("sp", dbg_pm, msb[:, :, 0:n], [("msb", g) for g in range(4)], ["dbg_pm"], P.new_dma_sem("dbgd_b"))
                dma("sp", dbg_mg, mg[:, :, 0:n], [("mg", c) for c in range(8)], ["dbg_mg"], P.new_dma_sem("dbgd_c"))
            c_o = {}
            for m in range(8):
                wv, wk, cs_ = wchunk("wo", l, 0, m, c_o)
                ps, pk = next_ps()
                mm(ps[:, 0:n], pk, [(wv[:, c, cs_], mg[:, c, 0:n]) for c in range(8)], [wk] + [("mg", c) for c in range(8)])
                dve(lambda e, ps=ps, m=m, t0=t0, n=n: e.scalar_tensor_tensor(
                    out=seg.X[:, m, t0:t0 + n], in0=ps[:, 0:n], scalar=dv[:, seg.col, 2, m:m + 1],
                    in1=seg.X[:, m, t0:t0 + n], op0=ALU.mult, op1=ALU.add),
                    [pk, ("dv", seg.col), xkey(seg, m, ti)], [xkey(seg, m, ti)])
            if debug and is_lat and ti == len(seg.tiles) - 1 and l == layers[0]:
                dsem2 = P.new_dma_sem("dbgdump2")
                dma("sp", dbg_xa, seg.X[:, :, t0:t0 + n], [xkey(seg, m, ti) for m in range(8)], ["dbg_xa"], dsem2)
            norm_mod(seg, ti, 1, mg, "mg")
            zk = [("mg", kc) for kc in range(8)]
            c_1 = {}

            def w1_part(j0, t0=t0, n=n, zk=zk, c_1=c_1):
                for j in range(j0, j0 + 4):
                    wv, wk, cs_ = wchunk("w1", l, 0, j, c_1)
                    ps, pk = next_ps()
                    mm(ps[:, 0:n], pk, [(wv[:, kc, cs_], mg[:, kc, 0:n]) for kc in range(8)], [wk] + zk)
                    r_ = nxt("relu", 2)
                    act(relu[:, r_, 0:n], ps[:, 0:n], AF.Relu, [pk], [("relu", r_)])
                    dve(lambda e, r_=r_, j=j, n=n: e.tensor_tensor(
                        out=HID[:, j, 0:n], in0=relu[:, r_, 0:n], in1=relu[:, r_, 0:n], op=ALU.mult),
                        [("relu", r_)], [bigkey(("hid", j))])

            def w2_part(m, t0=t0, n=n, ti=ti):
                w2a = wblock("w2", l, 2 * m)
                w2b = wblock("w2", l, 2 * m + 1)
                ps, pk = next_ps()
                terms = [(w2a[0][:, j, :], HID[:, j, 0:n]) for j in range(16)] + \
                        [(w2b[0][:, j, :], HID[:, 16 + j, 0:n]) for j in range(16)]
                mm(ps[:, 0:n], pk, terms, [w2a[1], w2b[1]] + [("hid", j) for j in range(32)])
                dve(lambda e, ps=ps, m=m, t0=t0, n=n: e.scalar_tensor_tensor(
                    out=seg.X[:, m, t0:t0 + n], in0=ps[:, 0:n], scalar=mod[:, 40 + m, seg.col:seg.col + 1],
                    in1=seg.X[:, m, t0:t0 + n], op0=ALU.mult, op1=ALU.add),
                    [pk, ("mod", seg.col), xkey(seg, m, ti)], [xkey(seg, m, ti)])

            for j0 in range(0, 32, 4):
                parts.append(lambda j0=j0: w1_part(j0))
            for m in range(8):
                parts.append(lambda m=m: w2_part(m))
        return parts

    out_sems = [P.new_dma_sem(f"outst{i}") for i in range(2)]

    def store_seg(seg, dst, do_norm):
        for ti, (t0, n) in enumerate(seg.tiles):
            nsub = n // 128
            if do_norm:
                rms_stats(seg, ti)
            for c in range(8):
                if do_norm:
                    dve(lambda e, c=c, t0=t0, n=n: e.scalar_tensor_tensor(
                        out=YN[:, c, 0:n], in0=seg.X[:, c, t0:t0 + n], scalar=vg[:, VG_FG + c:VG_FG + c + 1],
                        in1=rstd[:, 0:n], op0=ALU.mult, op1=ALU.mult),
                        [xkey(seg, c, ti), "rstd", "vg"], [bigkey(("yn", c))])
                else:
                    dve(lambda e, c=c, t0=t0, n=n: e.tensor_copy(out=YN[:, c, 0:n], in_=seg.X[:, c, t0:t0 + n]),
                        [xkey(seg, c, ti)], [bigkey(("yn", c))])
            for sbi in range(nsub):
                so = nxt("stage", 2)
                for half in range(2):
                    ps, pk = next_ps()

                    def fn(e, ps=ps, half=half, sbi=sbi):
                        ins = None
                        for cc in range(4):
                            c = half * 4 + cc
                            ins = e.transpose(ps[:, cc * 128:(cc + 1) * 128], YN[:, c, sbi * 128:(sbi + 1) * 128], ident[:])
                        return ins
                    P.add("pe", fn, reads=[("yn", c) for c in range(8)] + ["ident"], writes=[pk])
                    if half == 0:
                        dve(lambda e, ps=ps, so=so: e.tensor_copy(out=stage[:, so, 0:512], in_=ps[:, :]),
                            [pk], [bigkey(("stage", so, 0))])
                    else:
                        P.add("act", lambda e, ps=ps, so=so: e.copy(out=stage[:, so, 512:1024], in_=ps[:, :]),
                              reads=[pk], writes=[bigkey(("stage", so, 1))])
                dma("sp", dst[t0 + sbi * 128:t0 + (sbi + 1) * 128, :], stage[:, so, :], [("stage", so, 0), ("stage", so, 1)],
                    [("outrow", seg.name, ti, sbi)], out_sems[so])

    cst = sb("cst", [128, 2])
    dve(lambda e: e.memset(cst[:, 0:1], EPS), [], ["cst"])
    dve(lambda e: e.memset(cst[:, 1:2], 0.25), ["cst"], ["cst"])
    vg_eps = cst[:, 0:1]
    vg_q = cst[:, 1:2]

    load_seg(ctx, c_in)
    load_seg(lat, x_in)
    for c in range(8):
        for ti in range(len(lat.tiles)):
            ukey(lat, c, ti)
        bigkey(("yn", c))
    for j in range(32):
        bigkey(("hid", j))
    for so in range(2):
        for hf in range(2):
            bigkey(("stage", so, hf))
    bigkey(("Uhalo", "lat"))
    bigkey(("UhaloR", "lat"))
    try:
        chk("load")
        for li, l in enumerate(layers):
            last = (l == DEPTH - 1)
            layer_setup(l)
            chk("setup")
            build_convd()
            load_gw(l, 0)
            if li == 0:
                drain(prep_weights(l, "early"), 999)
            chk("prep")
            big_barrier()
            dve(lambda e: e.memset(U[:, :, 0:2], 0.0), [], [("Uhalo", "lat")])
            dve(lambda e: e.memset(CAR[:], 0.0), [], [("car", i, c) for i in range(3) for c in range(8)])
            phase1a(ctx, l, not last)
            chk("p1a_ctx")
            cparts = phase1b(ctx, l, not last, as_parts=True)
            lparts = phase1a(lat, l, True, as_parts=True)
            while cparts or lparts:
                drain(cparts, 1)
                drain(lparts, 1)
            chk("p1a_lat")
            dve(lambda e: e.tensor_copy(out=exch[:, :].rearrange("p (c k) -> p c k", k=2), in_=U[:, :, T:T + 2]),
                [ukey(lat, c, len(lat.tiles) - 1) for c in range(8)] + ["exch_in"], ["exch"])
            exchange()
            if li == 0:
                drain(prep_weights(l, "late"), 999)
            nlt = len(lat.tiles)
            phase1b(lat, l, True, tiles=list(range(nlt - 1)))
            exchange_finish()
            selv = selrow[:, :].rearrange("p (c k) -> p c k", k=2)
            for k in range(2):
                dve(lambda e, k=k: e.tensor_copy(out=U[:, :, T + 2 + k], in_=selv[:, :, 1 - k]),
                    ["selrow"], [("UhaloR", "lat")])
            chk("halo")
            phase1b(lat, l, True, tiles=[nlt - 1])
            chk("p1_lat")
            dve(lambda e: e.tensor_copy(out=exch[:, 0:8], in_=CAR[:, 0, :]),
                [("car", 0, c) for c in range(8)] + ["exch_in"], ["exch"])
            exchange()

            def apply_carry():
                exchange_finish()
                dve(lambda e: e.tensor_copy(out=CAR[:, 1, :], in_=selrow[:, 0:8]), ["selrow"],
                    [("car", 1, c) for c in range(8)])
            big_barrier()
            load_gw(l, 1)
            nextprep = prep_weights(layers[li + 1]) if li + 1 < len(layers) else []
            chk("carry")
            pending = []
            pre = [apply_carry]
            for sg, ti in [(lat, t_) for t_ in reversed(range(len(lat.tiles)))] + ([] if last else [(ctx, 0)]):
                if not (sg is lat and ti == len(lat.tiles) - 1):
                    drain(nextprep, 20)
                pending = phase2_tile(sg, l, ti, pending, pre)
            drain(pending, 99)
            drain(nextprep, 999)
            chk("layer%d" % l)
    except _Stop:
        pass
    big_barrier()
    if final:
        store_seg(lat, out_d, True)
    else:
        store_seg(lat, out_d, False)
        store_seg(ctx, outc_d, False)
    P.add("sp", lambda e: e.nop(), reads=[k for k in list(P.last_w.keys()) if isinstance(k, tuple) and k[0] == "outrow"],
          writes=["done"])
    P.emit(nc, st)
    st.close()
    return nc


def _fm(v):
    v = np.asarray(v, np.float32)
    return np.ascontiguousarray(v.reshape(-1, 128).T)


def _pool_matrix(L, w):
    t = np.arange(L)
    lo = np.clip(t - w // 2, 0, L)
    hi = np.clip(t + w - w // 2, 0, L)
    M = np.zeros((L, L), np.float32)
    for i in range(L):
        M[i, lo[i]:hi[i]] = np.float32(1.0) / np.float32(hi[i] - lo[i])
        M[i, i] -= 1.0
    return M


def _structural_constants(flip):
    wins = (2, 4, 8, 16)
    pm_lat = np.zeros((128, 4, 128), np.float32)
    pm_ctx = np.zeros((128, 4, 2, 256), np.float32)
    for g, w in enumerate(wins):
        M = _pool_matrix(64, w)
        Mc = _pool_matrix(256, w)
        if flip:
            M = M[::-1, ::-1]
            Mc = Mc[::-1, ::-1]
        blk = np.zeros((128, 128), np.float32)
        blk[:64, :64] = M
        blk[64:, 64:] = M
        pm_lat[:, g, :] = blk.T
        McT = np.ascontiguousarray(Mc.T)
        pm_ctx[:, g, 0, :] = McT[0:128]
        pm_ctx[:, g, 1, :] = McT[128:256]
    return pm_lat, pm_ctx


def _prepare(inputs):
    f = lambda k: np.asarray(inputs[k], np.float32)
    x, c, ctx, c_ctx = f("x"), f("c"), f("ctx"), f("c_ctx")
    shared = {
        "ident": np.eye(128, dtype=np.float32),
        "w_ada": f("w_ada"), "w_in": f("w_in"), "w_lru_out": f("w_lru_out"), "w_pool_out": f("w_pool_out"),
        "w_o": f("w_o"), "mlp_w1": f("mlp_w1"), "mlp_w2": f("mlp_w2"),
        "pool_w": np.ascontiguousarray(f("pool_w").transpose(0, 2, 1, 3)),
    }
    conv_w, conv_b = f("conv_w"), f("conv_b")
    w_r, w_i, b_r, b_i, lam = f("lru_w_r"), f("lru_w_i"), f("lru_b_r"), f("lru_b_i"), f("lru_lambda")
    per_half = []
    for h in range(2):
        dirs = (0, 1) if h == 0 else (1, 0)
        vec_l = np.zeros((DEPTH, 128, V_PER_LAYER), np.float32)
        gw = np.zeros((DEPTH, 128, 4, 8, 128), np.float32)
        for l in range(DEPTH):
            vec_l[l, :, V_N1G:V_N1G + 8] = _fm(inputs["norm1_g"][l])
            vec_l[l, :, V_N2G:V_N2G + 8] = _fm(inputs["norm2_g"][l])
            vec_l[l, :, V_CB:V_CB + 8] = _fm(conv_b[l])
            taps = np.zeros((5, D), np.float32)
            if h == 0:
                taps[0:4] = conv_w[l]
            else:
                taps[1:5] = conv_w[l][::-1]
            for j in range(5):
                vec_l[l, :, V_CW + j:V_CW + 40:5] = _fm(taps[j])
            for dl, d in enumerate(dirs):
                vec_l[l, :, V_BR + 8 * dl:V_BR + 8 * dl + 8] = _fm(b_r[l, d])
                vec_l[l, :, V_BI + 8 * dl:V_BI + 8 * dl + 8] = _fm(b_i[l, d])
                vec_l[l, :, V_LAM + 8 * dl:V_LAM + 8 * dl + 8] = _fm(lam[l, d])
                for gi, wsrc in enumerate((w_r, w_i)):
                    for ch in range(8):
                        for hh in range(2):
                            gw[l, hh * 64:(hh + 1) * 64, 2 * dl + gi, ch, hh * 64:(hh + 1) * 64] = wsrc[l, d, 2 * ch + hh]
            vec_l[l, :, V_PS:V_PS + 4] = _fm(inputs["pool_scale"][l])
            vec_l[l, :, V_BADA:V_BADA + 48] = _fm(inputs["b_ada"][l])
        pm_lat, pm_ctx = _structural_constants(h == 1)
        per_half.append(dict(vec_l=vec_l, gw=gw, pm_lat=pm_lat, pm_ctx=pm_ctx))
    in_maps = []
    for core in range(8):
        b, h = core // 2, core % 2
        xs = x[b, h * T:(h + 1) * T]
        cs = ctx[b]
        if h == 1:
            xs = xs[::-1]
            cs = cs[::-1]
        vec_g = np.zeros((128, V_GLOBAL), np.float32)
        vec_g[:, VG_FG:VG_FG + 8] = _fm(inputs["final_g"])
        vec_g[:, VG_C:VG_C + 8] = _fm(c[b])
        vec_g[:, VG_CC:VG_CC + 8] = _fm(c_ctx)
        vec_g[:, VG_SEL + (1 - h)] = 1.0
        m = dict(shared)
        m.update(per_half[h])
        m["x_in"] = np.ascontiguousarray(xs)
        m["c_in"] = np.ascontiguousarray(cs)
        m["vec_g"] = vec_g
        in_maps.append(m)
    return in_maps


_NC_CACHE = {}


def kernel(**inputs):
    in_maps = _prepare(inputs)
    key = ("full",)
    if key not in _NC_CACHE:
        _NC_CACHE[key] = build_nc((0, 1), True)
    res = run_bass_kernel_spmd(_NC_CACHE[key], in_maps, core_ids=list(range(8)))
    out = np.zeros((NB, SEQ, D), np.float32)
    for core in range(8):
        b, h = core // 2, core % 2
        o = res.results[core]["out"]
        if h == 1:
            o = o[::-1]
        out[b, h * T:(h + 1) * T] = o
    return out
```

```python
import numpy as np
from contextlib import ExitStack
import concourse.bass as bass
import concourse.mybir as mybir
from concourse.bass_utils import run_bass_kernel_spmd

F32 = mybir.dt.float32
BF16 = mybir.dt.bfloat16
AF = mybir.ActivationFunctionType
ALU = mybir.AluOpType

D = 1024
NCH = 8
NB = 4
SEQ = 4096
T = SEQ // 2
TC = 256
DEPTH = 2
D_IN = 4608
D_FF = 4096
EPS = 1e-6
NT = 512
SEM_CAP = 4000
UW = T + 4

V_N1G, V_N2G, V_CB, V_CW, V_BR, V_BI, V_LAM, V_PS, V_BADA = 0, 8, 16, 24, 64, 80, 96, 112, 116
V_PER_LAYER = 164
VG_FG, VG_C, VG_CC, VG_SEL = 0, 8, 16, 24
V_GLOBAL = 26


class Op:
    __slots__ = ("eng", "fn", "deps", "dma_sem", "token", "needed", "inc")

    def __init__(self, eng, fn, dma_sem, inc):
        self.eng, self.fn, self.dma_sem, self.inc = eng, fn, dma_sem, inc
        self.deps = set()
        self.token = None
        self.needed = False


class Prog:
    ENGS = ("pe", "act", "dve", "pool", "sp")

    def __init__(self):
        self.ops = []
        self.last_w = {}
        self.readers = {}
        self.dma_sem_names = []

    def new_dma_sem(self, name):
        self.dma_sem_names.append(name)
        return name

    def add(self, eng, fn, reads=(), writes=(), dma_sem=None, inc=None, waw=True):
        o = Op(eng, fn, dma_sem, inc if inc is not None else (16 if dma_sem else 1))
        for k in reads:
            w = self.last_w.get(k)
            if w is not None:
                o.deps.add(w)
        for k in writes:
            w = self.last_w.get(k)
            if w is not None and waw:
                o.deps.add(w)
            for r in self.readers.get(k, ()):
                o.deps.add(r)
        o.deps.discard(o)
        for k in reads:
            self.readers.setdefault(k, []).append(o)
        for k in writes:
            self.last_w[k] = o
            self.readers[k] = []
        self.ops.append(o)
        return o

    def emit(self, nc, stack):
        for o in self.ops:
            for d in o.deps:
                d.needed = True
        cnt = {e: 0 for e in self.ENGS}
        dcnt = {}
        for o in self.ops:
            if o.dma_sem is not None:
                dcnt[o.dma_sem] = dcnt.get(o.dma_sem, 0) + o.inc
                o.token = (o.dma_sem, dcnt[o.dma_sem])
            elif o.needed:
                cnt[o.eng] += 1
                k = cnt[o.eng]
                o.token = ((o.eng, (k - 1) // SEM_CAP), (k - 1) % SEM_CAP + 1)
        sems = {}
        for e in self.ENGS:
            for ep in range((cnt[e] + SEM_CAP - 1) // SEM_CAP):
                sems[(e, ep)] = stack.enter_context(nc.semaphore(f"s_{e}{ep}"))
        for n in self.dma_sem_names:
            if n in dcnt:
                sems[n] = stack.enter_context(nc.semaphore(f"d_{n}"))
        block = stack.enter_context(nc.Block())
        per_eng = {e: [o for o in self.ops if o.eng == e] for e in self.ENGS}

        def run(eng_name, eng):
            waited = {}
            for o in per_eng[eng_name]:
                need = {}
                for d in o.deps:
                    if d.dma_sem is None and d.eng == "pe" and eng_name == "pe":
                        continue
                    key, val = d.token
                    if need.get(key, 0) < val:
                        need[key] = val
                for key, val in need.items():
                    if waited.get(key, 0) >= val:
                        continue
                    eng.wait_ge(sems[key], val)
                    waited[key] = val
                ins = o.fn(eng)
                if o.token is not None:
                    if o.dma_sem is not None and o.inc == 1:
                        ins.then_inc(sems[o.token[0]])
                    else:
                        ins.then_inc(sems[o.token[0]], o.inc)

        @block.tensor
        def _(e):
            run("pe", e)

        @block.scalar
        def _(e):
            run("act", e)

        @block.vector
        def _(e):
            run("dve", e)

        @block.gpsimd
        def _(e):
            run("pool", e)

        @block.sync
        def _(e):
            run("sp", e)


def build_nc(layers=(0, 1), final=True, stop=None, debug=False, noexch=False):
    nc = bass.Bass("TRN2", target_bir_lowering=False)
    P = Prog()
    st = ExitStack()
    nl = len(layers)

    class _Stop(Exception):
        pass

    def chk(name):
        if stop == name:
            raise _Stop()

    def din(name, shape):
        return nc.dram_tensor(name, list(shape), F32, kind="ExternalInput").ap()

    x_in = din("x_in", [T, D])
    c_in = din("c_in", [TC, D])
    vec_g = din("vec_g", [128, V_GLOBAL])
    vec_l = din("vec_l", [DEPTH, 128, V_PER_LAYER])
    ident_d = din("ident", [128, 128])
    pml_d = din("pm_lat", [128, 4, 128])
    pmc_d = din("pm_ctx", [128, 4, 2, 256])
    gw_d = din("gw", [DEPTH, 128, 4, 8, 128])
    poolw_d = din("pool_w", [DEPTH, 128, 4, 128])
    w_ada_d = din("w_ada", [DEPTH, D, 6 * D])
    w_in_d = din("w_in", [DEPTH, D, D_IN])
    w_lo_d = din("w_lru_out", [DEPTH, D, D])
    w_po_d = din("w_pool_out", [DEPTH, 512, D])
    w_o_d = din("w_o", [DEPTH, D, D])
    w1_d = din("mlp_w1", [DEPTH, D, D_FF])
    w2_d = din("mlp_w2", [DEPTH, D_FF, D])
    out_d = nc.dram_tensor("out", [T, D], F32, kind="ExternalOutput").ap()
    outc_d = None
    if not final:
        outc_d = nc.dram_tensor("out_ctx", [TC, D], F32, kind="ExternalOutput").ap()

    def dscr(name, shape, dt):
        if debug and name.startswith("sp_"):
            return nc.dram_tensor(name, list(shape), dt, kind="ExternalOutput").ap()
        return nc.dram_tensor(name, list(shape), dt).ap()

    wsc = {}
    for l in layers:
        wsc[("win", l)] = dscr(f"s_win{l}", [18, 128, 8, 256], BF16)
        wsc[("wlo", l)] = dscr(f"s_wlo{l}", [4, 128, 8, 256], BF16)
        wsc[("wo", l)] = dscr(f"s_wo{l}", [4, 128, 8, 256], BF16)
        wsc[("wpo", l)] = dscr(f"s_wpo{l}", [2, 128, 4, 512], BF16)
        wsc[("w1", l)] = dscr(f"s_w1{l}", [16, 128, 8, 256], BF16)
        wsc[("w2", l)] = dscr(f"s_w2{l}", [16, 128, 16, 128], BF16)
    spill = {}
    for sname, tt in (("lat", T), ("ctx", TC)):
        spill[("h", sname)] = dscr(f"sp_h_{sname}", [128, 8, tt], BF16)
        spill[("uc", sname)] = dscr(f"sp_uc_{sname}", [128, 8, tt], F32)
        spill[("h1", sname)] = dscr(f"sp_h1_{sname}", [128, 8, tt], F32)
    if debug:
        dbg_z = nc.dram_tensor("dbg_z", [128, 8, NT], BF16, kind="ExternalOutput").ap()
        dbg_pm = nc.dram_tensor("dbg_pm", [128, 4, NT], BF16, kind="ExternalOutput").ap()
        dbg_mg = nc.dram_tensor("dbg_mg", [128, 8, NT], BF16, kind="ExternalOutput").ap()
        dbg_xa = nc.dram_tensor("dbg_xa", [128, 8, NT], F32, kind="ExternalOutput").ap()
    cc_src = [dscr(f"cc_src{i}", [128, 16], F32) for i in range(2 * nl)]
    cc_dst = [dscr(f"cc_dst{i}", [256, 16], F32) for i in range(2 * nl)]

    def sb(name, shape, dt=F32):
        return st.enter_context(nc.sbuf_tensor("sb_" + name, list(shape), dt))

    XL = sb("XL", [128, NCH, T])
    XC = sb("XC", [128, NCH, TC])
    BIG = sb("BIG", [128, NCH * UW], BF16)
    UCX = sb("UCX", [128, NCH, TC + 4], BF16)
    RING_N = 4
    RING = sb("RING", [128, RING_N, 2048], BF16)
    ident = sb("ident", [128, 128])
    onesb = sb("onesb", [128, 128], BF16)
    pml = sb("pml", [128, 4, 128], BF16)
    pmc = sb("pmc", [128, 4, 2, 256], BF16)
    vg = sb("vg", [128, V_GLOBAL])
    vl = sb("vl", [128, V_PER_LAYER])
    gwd = sb("gwd", [128, 2, 8, 128], BF16)
    poolw = sb("poolw", [128, 4, 128], BF16)
    silc = sb("silc", [128, 8, 2], BF16)
    mod = sb("mod", [128, 48, 2])
    dv = sb("dv", [128, 2, 3, 8])
    lruc = sb("lruc", [128, 2, 4, 8])
    tmpv = sb("tmpv", [128, 2, 8])
    CAR = sb("CAR", [128, 3, 8])
    exch = sb("exch", [128, 16])
    exch_in = sb("exch_in", [128, 2, 16])
    selrow = sb("selrow", [128, 16])
    hb = sb("hb", [128, NCH, NT], BF16)
    zb = sb("zb", [128, NCH, NT], BF16)
    mgflat = sb("mg", [128, NCH * NT], BF16)
    sq = sb("sq", [128, 2, NT], BF16)
    rstd = sb("rstd", [128, NT])
    tmp32 = sb("tmp32", [128, 2, NT])
    ucf = sb("ucf", [128, 2, NT])
    ucb = sb("ucb", [128, 2, NT], BF16)
    h1t = sb("h1t", [128, 2, NT])
    thr = sb("thr", [128, 2, NT])
    thi = sb("thi", [128, 2, NT])
    at = sb("at", [128, 2, NT])
    a2t = sb("a2t", [128, 2, NT])
    pT = sb("pT", [128, 4, 512], BF16)
    msb = sb("msb", [128, 4, NT], BF16)
    relu = sb("relu", [128, 2, NT], BF16)
    psum = [st.enter_context(nc.psum_tensor(f"ps{i}", [128, 512], F32)) for i in range(8)]

    mg = mgflat[:, :].rearrange("p (c t) -> p c t", c=NCH)
    pTflat = pT[:, :, :].rearrange("p a b -> p (a b)")
    CD = [mgflat[:, c * 640:(c + 1) * 640].rearrange("p (j e) -> p j e", j=5) for c in range(6)] + \
         [pTflat[:, c * 640:(c + 1) * 640].rearrange("p (j e) -> p j e", j=5) for c in range(2)]
    CDKEYS = [("mg", c) for c in range(8)] + [("pT", s_) for s_ in range(4)]
    U = BIG[:, :].rearrange("p (c t) -> p c t", c=NCH)
    HID = BIG[:, 0:32 * NT].rearrange("p (j t) -> p j t", j=32)
    BIGF = BIG[:, :].bitcast(F32)
    STG = BIGF[:, 0:4096].rearrange("p (a b) -> p a b", a=4)
    YN = BIGF[:, 0:4096].rearrange("p (a b) -> p a b", a=8)
    stage = BIGF[:, 4096:6144].rearrange("p (a b) -> p a b", a=2)
    BIGKEYS = ["bigbar"]

    ps_ctr = [0]

    def next_ps():
        i = ps_ctr[0] % 8
        ps_ctr[0] += 1
        return psum[i], ("ps", i)

    rot = {}

    def nxt(name, n):
        i = rot.get(name, 0)
        rot[name] = i + 1
        return i % n

    def dma(eng, out, in_, reads, writes, sem):
        return P.add(eng, lambda e: e.dma_start(out=out, in_=in_), reads=reads, writes=writes, dma_sem=sem)

    def act(out, in_, func, reads, writes, scale=1.0, bias=0.0):
        return P.add("act", lambda e: e.activation(out=out, in_=in_, func=func, bias=bias, scale=scale),
                     reads=reads, writes=writes)

    def dve(fn, reads, writes):
        return P.add("dve", fn, reads=reads, writes=writes)

    def mm(ps_ap, ps_key, terms, reads):
        def fn(e):
            n = len(terms)
            ins = None
            for i, (l_, r_) in enumerate(terms):
                ins = e.matmul(ps_ap, lhsT=l_, rhs=r_, start=(i == 0), stop=(i == n - 1))
            return ins
        return P.add("pe", fn, reads=reads, writes=[ps_key])

    def big_barrier():
        keys = list(BIGKEYS)
        dve(lambda e: e.memset(selrow[:, 0:1], 0.0), keys, keys + ["selrow"])

    def bigkey(k):
        if k not in BIGKEYS:
            BIGKEYS.append(k)
        return k

    s_const = P.new_dma_sem("const")
    s_const2 = P.new_dma_sem("const2")
    s_const3 = P.new_dma_sem("const3")
    s_const4 = P.new_dma_sem("const4")
    dma("sp", ident[:], ident_d, [], ["ident"], s_const)
    dma("sp", vg[:], vec_g, [], ["vg"], s_const2)
    dma("pool", pml[:], pml_d, [], ["pml"], s_const3)
    dma("pool", pmc[:], pmc_d, [], ["pmc"], s_const4)
    dve(lambda e: e.memset(onesb[:], 1.0 / D), [], ["onesb"])
    dve(lambda e: e.memset(UCX[:], 0.0), [], ["ucx"])
    act(silc[:, :, 0], vg[:, VG_C:VG_C + 8], AF.Silu, ["vg"], ["silc"])
    act(silc[:, :, 1], vg[:, VG_CC:VG_CC + 8], AF.Silu, ["vg", "silc"], ["silc"])

    wgroup = {}

    def prep_weights(l, which="all"):
        todo = []

        def conv(name, src, j0, j1, cw, kcn=8, gsz=4):
            dst = wsc[(name, l)]
            for j in range(j0, j1):
                grp = j // gsz
                sem = f"w_{name}{l}_{grp}"
                if sem not in P.dma_sem_names:
                    P.new_dma_sem(sem)
                if name == "w2":
                    m, hf = j // 2, j % 2
                    srcv = src[:, m * cw:(m + 1) * cw].rearrange("(k p) c -> p k c", p=128)[:, hf * 16:(hf + 1) * 16, :]
                else:
                    srcv = src[:, j * cw:(j + 1) * cw].rearrange("(k p) c -> p k c", p=128)[:, 0:kcn, :]
                wgroup[(name, l, j)] = ("wsc", name, l, grp)
                todo.append(lambda o_=dst[j], i_=srcv, sem=sem, key=("wsc", name, l, grp): P.add(
                    "pool", lambda e: e.dma_start(out=o_, in_=i_), reads=[], writes=[key], dma_sem=sem, waw=False))
        if which in ("all", "early"):
            conv("win", w_in_d[l], 0, 4, 256)
            conv("win", w_in_d[l], 4, 18, 256)
            conv("wlo", w_lo_d[l], 0, 4, 256)
            conv("wpo", w_po_d[l], 0, 2, 512, 4)
            conv("wo", w_o_d[l], 0, 4, 256)
        if which in ("all", "late"):
            conv("w1", w1_d[l], 0, 16, 256)
            conv("w2", w2_d[l], 0, 16, 128, 16)
        return todo

    ring_sems = [P.new_dma_sem(f"ring{i}") for i in range(RING_N)]

    def wblock(name, l, j):
        s_ = nxt("ring", RING_N)
        key = ("ring", s_)
        if name == "w2":
            view = RING[:, s_, :].rearrange("p (k c) -> p k c", k=16)
        elif name == "wpo":
            view = RING[:, s_, :].rearrange("p (k c) -> p k c", k=4)
        else:
            view = RING[:, s_, :].rearrange("p (k c) -> p k c", k=8)
        dma("sp", view, wsc[(name, l)][j], [wgroup[(name, l, j)]], [key], ring_sems[s_])
        return view, key

    def wchunk(name, l, base, m, cache):
        jb = base + m // 2
        if cache.get("jb") != (name, jb):
            cache["jb"] = (name, jb)
            cache["w"] = wblock(name, l, jb)
        wv, wk = cache["w"]
        return wv, wk, slice((m % 2) * 128, (m % 2) * 128 + 128)

    ada_sems = [P.new_dma_sem(f"ada{i}") for i in range(4)]
    gw_sems = [P.new_dma_sem(f"gw{i}") for i in range(2)]

    def load_gw(l, d):
        dma("pool", gwd[:], gw_d[l][:, 2 * d:2 * d + 2, :, :], [], ["gwd"], gw_sems[d])

    def layer_setup(l):
        s_lv = P.new_dma_sem(f"lv{l}")
        s_lv2 = P.new_dma_sem(f"lvp{l}")
        dma("sp", vl[:], vec_l[l], [], ["vl"], s_lv)
        dma("pool", poolw[:], poolw_d[l], [], ["poolw"], s_lv2)
        psm, pk = next_ps()
        hbflat = hb[:, :, :].rearrange("p c t -> p (c t)")
        for jb in range(24):
            s_ = nxt("ada", 4)
            flat, kname = (mgflat, "mg") if s_ < 2 else (hbflat, "hb")
            view = flat[:, (s_ % 2) * 2048:(s_ % 2 + 1) * 2048].rearrange("p (k c) -> p k c", k=8)
            akeys = [(kname, c) for c in range(4 * (s_ % 2), 4 * (s_ % 2) + 4)]
            dma("pool", view, w_ada_d[l][:, jb * 256:(jb + 1) * 256].rearrange("(k p) c -> p k c", p=128),
                [], akeys, ada_sems[s_])
            for jj in range(2):
                j = jb * 2 + jj

                def fn(e, view=view, jj=jj, j=j):
                    ins = None
                    for kc in range(8):
                        ins = e.matmul(psm[:, 2 * j:2 * j + 2], lhsT=view[:, kc, jj * 128:(jj + 1) * 128],
                                       rhs=silc[:, kc, :], start=(kc == 0), stop=(kc == 7))
                    return ins
                P.add("pe", fn, reads=akeys + ["silc"], writes=[pk] if j == 0 else [(pk, "part")])
        psv = psm[:, 0:96].rearrange("p (j t) -> p j t", t=2)
        for col in range(2):
            dve(lambda e, col=col: e.tensor_tensor(out=mod[:, :, col], in0=psv[:, :, col],
                                                   in1=vl[:, V_BADA:V_BADA + 48], op=ALU.add),
                [pk, (pk, "part"), "vl"], [("mod", col)])
            dve(lambda e, col=col: e.scalar_tensor_tensor(
                out=dv[:, col, 0, :], in0=mod[:, 8:16, col], scalar=1.0, in1=vl[:, V_N1G:V_N1G + 8],
                op0=ALU.add, op1=ALU.mult), [("mod", col), "vl"], [("dv", col)])
            dve(lambda e, col=col: e.scalar_tensor_tensor(
                out=dv[:, col, 1, :], in0=mod[:, 32:40, col], scalar=1.0, in1=vl[:, V_N2G:V_N2G + 8],
                op0=ALU.add, op1=ALU.mult), [("mod", col), "vl", ("dv", col)], [("dv", col)])
            dve(lambda e, col=col: e.tensor_scalar(
                out=dv[:, col, 2, :], in0=mod[:, 16:24, col], scalar1=0.5, scalar2=None, op0=ALU.mult),
                [("mod", col), ("dv", col)], [("dv", col)])
        for d in range(2):
            act(tmpv[:, d, :], vl[:, V_LAM + 8 * d:V_LAM + 8 * d + 8], AF.Exp, ["vl"], [("tmpv", d)], scale=-1.0)
            act(tmpv[:, d, :], tmpv[:, d, :], AF.Ln, [("tmpv", d)], [("tmpv", d)], bias=1.0)
            for k, (srcap, mul) in enumerate((
                    (tmpv[:, d, :], -4.0), (tmpv[:, d, :], -8.0),
                    (vl[:, V_BR + 8 * d:V_BR + 8 * d + 8], 0.5), (vl[:, V_BI + 8 * d:V_BI + 8 * d + 8], 0.5))):
                dve(lambda e, d=d, k=k, srcap=srcap, mul=mul: e.tensor_scalar(
                    out=lruc[:, d, k, :], in0=srcap, scalar1=mul, scalar2=None, op0=ALU.mult),
                    [("tmpv", d), "vl", ("lruc", d)], [("lruc", d)])

    class Seg:
        pass

    lat = Seg()
    lat.name, lat.X, lat.T, lat.col, lat.U = "lat", XL, T, 0, U
    lat.tiles = [(i * NT, NT) for i in range(T // NT)]
    lat.car2 = 1
    ctx = Seg()
    ctx.name, ctx.X, ctx.T, ctx.col, ctx.U = "ctx", XC, TC, 1, UCX
    ctx.tiles = [(0, TC)]
    ctx.car2 = 2

    def xkey(seg, c, ti):
        return ("x", seg.name, c, ti)

    def ukey(seg, c, ti):
        k = ("U", seg.name, c, ti)
        return bigkey(k) if seg is lat else k

    stg_sems = [P.new_dma_sem(f"stg{i}") for i in range(4)]

    def load_seg(seg, src):
        for ti, (t0, n) in enumerate(seg.tiles):
            nsub = n // 128
            for sbi in range(nsub):
                dma("sp", STG[:, sbi, :], src[t0 + sbi * 128:t0 + (sbi + 1) * 128, :], [], [bigkey(("stg", sbi))],
                    stg_sems[sbi])
            for c in range(8):
                ps, pk = next_ps()

                def fn(e, ps=ps, c=c, nsub=nsub):
                    ins = None
                    for sbi in range(nsub):
                        ins = e.transpose(ps[:, sbi * 128:(sbi + 1) * 128], STG[:, sbi, c * 128:(c + 1) * 128], ident[:])
                    return ins
                P.add("pe", fn, reads=[("stg", s) for s in range(nsub)] + ["ident"], writes=[pk])
                if c % 2 == 0:
                    dve(lambda e, ps=ps, c=c, t0=t0, n=n: e.tensor_copy(out=seg.X[:, c, t0:t0 + n], in_=ps[:, 0:n]),
                        [pk], [xkey(seg, c, ti)])
                else:
                    P.add("act", lambda e, ps=ps, c=c, t0=t0, n=n: e.copy(out=seg.X[:, c, t0:t0 + n], in_=ps[:, 0:n]),
                          reads=[pk], writes=[xkey(seg, c, ti)])

    def rms_stats(seg, ti):
        t0, n = seg.tiles[ti]
        ps, pk = next_ps()
        for c in range(8):
            s_ = nxt("sq", 2)
            act(sq[:, s_, 0:n], seg.X[:, c, t0:t0 + n], AF.Square, [xkey(seg, c, ti)], [("sq", s_)])

            def fn(e, s_=s_, c=c, ps=ps, n=n):
                return e.matmul(ps[:, 0:n], lhsT=onesb[:], rhs=sq[:, s_, 0:n], start=(c == 0), stop=(c == 7))
            P.add("pe", fn, reads=[("sq", s_), "onesb"], writes=[pk] if c == 0 else [(pk, "acc")])
        act(rstd[:, 0:n], ps[:, 0:n], AF.Sqrt, [pk, (pk, "acc")], ["rstd"], bias=vg_eps)
        dve(lambda e, n=n: e.reciprocal(out=rstd[:, 0:n], in_=rstd[:, 0:n]), ["rstd"], ["rstd"])

    def norm_mod(seg, ti, which, dst, dkey):
        t0, n = seg.tiles[ti]
        col = seg.col
        rms_stats(seg, ti)
        sh0 = 0 if which == 0 else 24
        for c in range(8):
            s_ = nxt("tmp32", 2)
            dve(lambda e, s_=s_, c=c, t0=t0, n=n: e.tensor_tensor(
                out=tmp32[:, s_, 0:n], in0=seg.X[:, c, t0:t0 + n], in1=rstd[:, 0:n], op=ALU.mult),
                [xkey(seg, c, ti), "rstd"], [("tmp32", s_)])
            act(dst[:, c, 0:n], tmp32[:, s_, 0:n], AF.Identity, [("tmp32", s_), ("dv", col), ("mod", col)],
                [(dkey, c)], scale=dv[:, col, which, c:c + 1], bias=mod[:, sh0 + c, col:col + 1])

    def lru_front(d, c, n, us):
        s_ = nxt("lru", 2)
        ps_r, pkr = next_ps()
        ps_i, pki = next_ps()
        mm(ps_r[:, 0:n], pkr, [(gwd[:, 0, c, :], ucb[:, us, 0:n])], ["gwd", ("ucb", us)])
        mm(ps_i[:, 0:n], pki, [(gwd[:, 1, c, :], ucb[:, us, 0:n])], ["gwd", ("ucb", us)])
        act(thr[:, s_, 0:n], ps_r[:, 0:n], AF.Tanh, [pkr, ("lruc", d)], [("thr", s_)], scale=0.5,
            bias=lruc[:, d, 2, c:c + 1])
        act(thi[:, s_, 0:n], ps_i[:, 0:n], AF.Tanh, [pki, ("lruc", d)], [("thi", s_)], scale=0.5,
            bias=lruc[:, d, 3, c:c + 1])
        act(at[:, s_, 0:n], thr[:, s_, 0:n], AF.Exp, [("thr", s_), ("lruc", d)], [("at", s_)],
            scale=lruc[:, d, 0, c:c + 1], bias=lruc[:, d, 0, c:c + 1])
        act(a2t[:, s_, 0:n], thr[:, s_, 0:n], AF.Exp, [("thr", s_), ("lruc", d)], [("a2t", s_)],
            scale=lruc[:, d, 1, c:c + 1], bias=lruc[:, d, 1, c:c + 1])
        dve(lambda e: e.scalar_tensor_tensor(out=thi[:, s_, 0:n], in0=thi[:, s_, 0:n], scalar=1.0,
                                             in1=ucf[:, us, 0:n], op0=ALU.add, op1=ALU.mult),
            [("thi", s_), ("ucf", us)], [("thi", s_)])
        return s_

    def lru_back(s_, n):
        act(a2t[:, s_, 0:n], a2t[:, s_, 0:n], AF.Sqrt, [("a2t", s_)], [("a2t", s_)], scale=-0.25, bias=vg_q)
        dve(lambda e: e.tensor_tensor(out=thi[:, s_, 0:n], in0=thi[:, s_, 0:n], in1=a2t[:, s_, 0:n], op=ALU.mult),
            [("thi", s_), ("a2t", s_)], [("thi", s_)])

    spill_sems = {}

    def ssem(name):
        if name not in spill_sems:
            spill_sems[name] = P.new_dma_sem(name)
        return spill_sems[name]

    def phase1a(seg, l, need_p2, as_parts=False):
        def tile_part(ti):
            t0, n = seg.tiles[ti]
            which = nxt("p1buf", 2)
            dst, dkey = (hb, "hb") if which == 0 else (zb, "zb")
            norm_mod(seg, ti, 0, dst, dkey)
            hk = [(dkey, c) for c in range(8)]
            if need_p2:
                dma("sp", spill[("h", seg.name)][:, :, t0:t0 + n], dst[:, :, 0:n], hk, [("sp_h", seg.name, ti)],
                    ssem(f"sph{which}"))
            cache = {}
            for m in range(8):
                wv, wk, cs_ = wchunk("win", l, 0, m, cache)
                ps, pk = next_ps()
                mm(ps[:, 0:n], pk, [(wv[:, kc, cs_], dst[:, kc, 0:n]) for kc in range(8)], [wk] + hk)
                dve(lambda e, ps=ps, m=m, t0=t0, n=n: e.tensor_copy(out=seg.U[:, m, 2 + t0:2 + t0 + n], in_=ps[:, 0:n]),
                    [pk], [ukey(seg, m, ti)])
        parts = [(lambda ti=ti: tile_part(ti)) for ti in range(len(seg.tiles))]
        if as_parts:
            return parts
        for p_ in parts:
            p_()

    def build_convd():
        for c in range(8):
            for j in range(5):
                dve(lambda e, c=c, j=j: e.tensor_scalar(
                    out=CD[c][:, j, :], in0=ident[:], scalar1=vl[:, V_CW + c * 5 + j:V_CW + c * 5 + j + 1],
                    scalar2=None, op0=ALU.mult), ["ident", "vl"] + (CDKEYS if (c, j) == (0, 0) else []),
                    CDKEYS if (c, j) in ((0, 0), (7, 4)) else [("cdpart", c, j)])

    def phase1b(seg, l, need_p2, tiles=None, as_parts=False):
        ntile = len(seg.tiles)

        def f1(ti, c):
            t0, n = seg.tiles[ti]
            ps, pk = next_ps()
            rd = [ukey(seg, c, ti)] + CDKEYS
            if ti > 0:
                rd.append(ukey(seg, c, ti - 1))
            if ti + 1 < ntile:
                rd.append(ukey(seg, c, ti + 1))
            if seg is lat:
                if ti == 0:
                    rd.append(bigkey(("Uhalo", "lat")))
                if ti == ntile - 1:
                    rd.append(bigkey(("UhaloR", "lat")))
            else:
                rd.append("ucx")
            mm(ps[:, 0:n], pk, [(CD[c][:, j, :], seg.U[:, c, t0 + j:t0 + j + n]) for j in range(5)], rd)
            us = nxt("ucf", 2)
            dve(lambda e, ps=ps, us=us, c=c, n=n: e.tensor_scalar(
                out=ucf[:, us, 0:n], in0=ps[:, 0:n], scalar1=vl[:, V_CB + c:V_CB + c + 1], scalar2=None, op0=ALU.add),
                [pk, "vl"], [("ucf", us)])
            act(ucb[:, us, 0:n], ucf[:, us, 0:n], AF.Identity, [("ucf", us)], [("ucb", us)])
            if need_p2:
                dma("sp", spill[("uc", seg.name)][:, c, t0:t0 + n], ucf[:, us, 0:n], [("ucf", us)],
                    [("sp_uc", seg.name, c, ti)], ssem(f"spuc{us}"))
            return us

        def back(ti, c, s_):
            t0, n = seg.tiles[ti]
            lru_back(s_, n)
            hs_ = nxt("h1t", 2)
            dve(lambda e, s_=s_, hs_=hs_, c=c, n=n: e.tensor_tensor_scan(
                out=h1t[:, hs_, 0:n], data0=at[:, s_, 0:n], data1=thi[:, s_, 0:n], initial=CAR[:, 0, c:c + 1],
                op0=ALU.mult, op1=ALU.add), [("at", s_), ("thi", s_), ("car", 0, c)], [("h1t", hs_)])
            dve(lambda e, hs_=hs_, c=c, n=n: e.tensor_copy(out=CAR[:, 0, c:c + 1], in_=h1t[:, hs_, n - 1:n]),
                [("h1t", hs_)], [("car", 0, c)])
            if need_p2:
                dma("sp", spill[("h1", seg.name)][:, c, t0:t0 + n], h1t[:, hs_, 0:n], [("h1t", hs_)],
                    [("sp_h1", seg.name, c, ti)], ssem(f"sph1{hs_}"))

        items = [(ti, c) for ti in (range(ntile) if tiles is None else tiles) for c in range(8)]
        def pair(a, b):
            ua = f1(*a)
            ub = f1(*b)
            sa = lru_front(0, a[1], seg.tiles[a[0]][1], ua)
            sb_ = lru_front(0, b[1], seg.tiles[b[0]][1], ub)
            back(*a, sa)
            back(*b, sb_)
        parts = [(lambda a=items[i], b=items[i + 1]: pair(a, b)) for i in range(0, len(items), 2)]
        if as_parts:
            return parts
        for p_ in parts:
            p_()

    cc_ctr = [0]

    def exchange():
        i = cc_ctr[0]
        cc_ctr[0] += 1
        s1 = P.new_dma_sem(f"cca{i}")
        s2 = P.new_dma_sem(f"ccb{i}")
        s3 = P.new_dma_sem(f"ccc{i}")
        dma("pool", cc_src[i], exch[:], ["exch"], [("ccsrc", i)], s1)
        if noexch:
            dma("pool", cc_dst[i][0:128, :], cc_src[i], [("ccsrc", i)], [("ccdst", i)], s2)
            dma("pool", cc_dst[i][128:256, :], cc_src[i], [("ccsrc", i)], [("ccdst", i)], P.new_dma_sem(f"ccd{i}"))
        else:
            P.add("pool", lambda e: e.collective_compute(
                "AllGather", ALU.bypass, replica_groups=[[0, 1], [2, 3], [4, 5], [6, 7]],
                ins=[cc_src[i].opt()], outs=[cc_dst[i].opt()]), reads=[("ccsrc", i)], writes=[("ccdst", i)],
                dma_sem=s2, inc=1)
        dma("pool", exch_in[:], cc_dst[i].rearrange("(r p) k -> p r k", p=128), [("ccdst", i)], ["exch_in"], s3)

    def exchange_finish():
        dve(lambda e: e.tensor_scalar(out=selrow[:], in0=exch_in[:, 0, :], scalar1=vg[:, VG_SEL:VG_SEL + 1],
                                      scalar2=None, op0=ALU.mult), ["exch_in", "vg"], ["selrow"])
        dve(lambda e: e.scalar_tensor_tensor(out=selrow[:], in0=exch_in[:, 1, :], scalar=vg[:, VG_SEL + 1:VG_SEL + 2],
                                             in1=selrow[:], op0=ALU.mult, op1=ALU.add),
            ["exch_in", "vg", "selrow"], ["selrow"])

    def drain(pending, k):
        for _ in range(min(k, len(pending))):
            pending.pop(0)()

    def phase2_tile(seg, l, ti_, pending, pre):
        is_lat = seg is lat
        parts = []
        for ti in (ti_,):
            t0, n = seg.tiles[ti]
            nsub = n // 128
            hbk = [("hb", kc) for kc in range(8)]
            dma("sp", hb[:, :, 0:n], spill[("h", seg.name)][:, :, t0:t0 + n], [("sp_h", seg.name, ti)], hbk, ssem("ldh"))
            cache = {}
            for c in range(8):
                wv, wk, cs_ = wchunk("win", l, 4, c, cache)
                ps, pk = next_ps()
                mm(ps[:, 0:n], pk, [(wv[:, kc, cs_], hb[:, kc, 0:n]) for kc in range(8)], [wk] + hbk)
                act(zb[:, c, 0:n], ps[:, 0:n], AF.Gelu, [pk], [("zb", c)])
            drain(pending, 3)
            while pre:
                pre.pop(0)()

            def f1(c):
                us = nxt("ucf", 2)
                dma("sp", ucf[:, us, 0:n], spill[("uc", seg.name)][:, c, t0:t0 + n], [("sp_uc", seg.name, c, ti)],
                    [("ucf", us)], ssem(f"lduc{us}"))
                hs_ = nxt("h1t", 2)
                dma("sp", h1t[:, hs_, 0:n], spill[("h1", seg.name)][:, c, t0:t0 + n], [("sp_h1", seg.name, c, ti)],
                    [("h1t", hs_)], ssem(f"ldh1{hs_}"))
                act(ucb[:, us, 0:n], ucf[:, us, 0:n], AF.Identity, [("ucf", us)], [("ucb", us)])
                return us, hs_

            def back(c, s_, hs_):
                lru_back(s_, n)
                ck = ("car", seg.car2, c)
                dve(lambda e, s_=s_, c=c: e.tensor_tensor_scan(
                    out=thr[:, s_, 0:n][:, ::-1], data0=at[:, s_, 0:n][:, ::-1], data1=thi[:, s_, 0:n][:, ::-1],
                    initial=CAR[:, seg.car2, c:c + 1], op0=ALU.mult, op1=ALU.add),
                    [("at", s_), ("thi", s_), ck, ("thr", s_)], [("thr", s_)])
                dve(lambda e, s_=s_, c=c: e.tensor_copy(out=CAR[:, seg.car2, c:c + 1], in_=thr[:, s_, 0:1]),
                    [("thr", s_)], [ck])
                dve(lambda e, s_=s_, hs_=hs_: e.tensor_tensor(
                    out=thr[:, s_, 0:n], in0=thr[:, s_, 0:n], in1=h1t[:, hs_, 0:n], op=ALU.add),
                    [("thr", s_), ("h1t", hs_)], [("thr", s_)])
                dve(lambda e, s_=s_, c=c: e.tensor_tensor(
                    out=zb[:, c, 0:n], in0=thr[:, s_, 0:n], in1=zb[:, c, 0:n], op=ALU.mult),
                    [("thr", s_), ("zb", c)], [("zb", c)])

            for c0 in range(0, 8, 2):
                ua, ha = f1(c0)
                ub, hb_ = f1(c0 + 1)
                sa = lru_front(1, c0, n, ua)
                sb_ = lru_front(1, c0 + 1, n, ub)
                back(c0, sa, ha)
                back(c0 + 1, sb_, hb_)
                drain(pending, 3 if c0 < 6 else 16)
            wp = [wblock("win", l, 8), wblock("win", l, 9)]
            for sbi in range(nsub):
                ps, pk = next_ps()

                def fn(e, ps=ps, sbi=sbi, wp=wp):
                    ins = None
                    for hf in range(2):
                        for kc in range(8):
                            ins = e.matmul(ps[:, hf * 256:(hf + 1) * 256], lhsT=hb[:, kc, sbi * 128:(sbi + 1) * 128],
                                           rhs=wp[hf][0][:, kc, :], start=(kc == 0), stop=(kc == 7))
                    return ins
                P.add("pe", fn, reads=[wp[0][1], wp[1][1]] + hbk, writes=[pk])
                if sbi % 2 == 0:
                    dve(lambda e, ps=ps, sbi=sbi: e.tensor_copy(out=pT[:, sbi, :], in_=ps[:, :]), [pk], [("pT", sbi)])
                else:
                    P.add("act", lambda e, ps=ps, sbi=sbi: e.copy(out=pT[:, sbi, :], in_=ps[:, :]),
                          reads=[pk], writes=[("pT", sbi)])
            for g in range(4):
                ps, pk = next_ps()

                def fn(e, ps=ps, g=g, nsub=nsub, lat_=is_lat):
                    ins = None
                    if lat_:
                        for sbi in range(nsub):
                            ins = e.matmul(ps[:, sbi * 128:(sbi + 1) * 128], lhsT=pT[:, sbi, g * 128:(g + 1) * 128],
                                           rhs=pml[:, g, :], start=True, stop=True)
                    else:
                        for b_ in range(2):
                            ins = e.matmul(ps[:, 0:256], lhsT=pT[:, b_, g * 128:(g + 1) * 128], rhs=pmc[:, g, b_, :],
                                           start=(b_ == 0), stop=(b_ == 1))
                    return ins
                P.add("pe", fn, reads=[("pT", s) for s in range(nsub)] + ["pml", "pmc"], writes=[pk])
                dve(lambda e, ps=ps, g=g, n=n: e.tensor_copy(out=msb[:, g, 0:n], in_=ps[:, 0:n]), [pk], [("msb", g)])
                ps2, pk2 = next_ps()
                mm(ps2[:, 0:n], pk2, [(poolw[:, g, :], msb[:, g, 0:n])], ["poolw", ("msb", g)])
                act(msb[:, g, 0:n], ps2[:, 0:n], AF.Identity, [pk2, "vl"], [("msb", g)], scale=vl[:, V_PS + g:V_PS + g + 1])
            c_lo, c_a, c_b = {}, {}, {}
            wpo = None
            for m in range(8):
                if m % 2 == 0:
                    wpo = wblock("wpo", l, m // 4)
                wv, wk, cs_ = wchunk("wlo", l, 0, m, c_lo)
                psA, pkA = next_ps()
                mm(psA[:, 0:n], pkA, [(wv[:, c, cs_], zb[:, c, 0:n]) for c in range(8)], [wk] + [("zb", c) for c in range(8)])
                psB, pkB = next_ps()
                cs4 = slice((m % 4) * 128, (m % 4) * 128 + 128)
                mm(psB[:, 0:n], pkB, [(wpo[0][:, g, cs4], msb[:, g, 0:n]) for g in range(4)],
                   [wpo[1]] + [("msb", g) for g in range(4)])
                wv, wk, cs_ = wchunk("win", l, 10, m, c_a)
                psC, pkC = next_ps()
                mm(psC[:, 0:n], pkC, [(wv[:, kc, cs_], hb[:, kc, 0:n]) for kc in range(8)], [wk] + hbk)
                wv, wk, cs_ = wchunk("win", l, 14, m, c_b)
                psD, pkD = next_ps()
                mm(psD[:, 0:n], pkD, [(wv[:, kc, cs_], hb[:, kc, 0:n]) for kc in range(8)], [wk] + hbk)
                act(tmp32[:, 0, 0:n], psC[:, 0:n], AF.Tanh, [pkC], [("tmp32", 0)], scale=0.5)
                act(tmp32[:, 1, 0:n], psD[:, 0:n], AF.Tanh, [pkD], [("tmp32", 1)], scale=0.5)
                dve(lambda e, psA=psA, n=n: e.scalar_tensor_tensor(
                    out=tmp32[:, 0, 0:n], in0=tmp32[:, 0, 0:n], scalar=1.0, in1=psA[:, 0:n], op0=ALU.add, op1=ALU.mult),
                    [("tmp32", 0), pkA], [("tmp32", 0)])
                dve(lambda e, psB=psB, n=n: e.scalar_tensor_tensor(
                    out=tmp32[:, 1, 0:n], in0=tmp32[:, 1, 0:n], scalar=1.0, in1=psB[:, 0:n], op0=ALU.add, op1=ALU.mult),
                    [("tmp32", 1), pkB], [("tmp32", 1)])
                dve(lambda e, m=m, n=n: e.tensor_tensor(
                    out=mg[:, m, 0:n], in0=tmp32[:, 0, 0:n], in1=tmp32[:, 1, 0:n], op=ALU.add),
                    [("tmp32", 0), ("tmp32", 1)], [("mg", m)])
            if debug and is_lat and ti == len(seg.tiles) - 1 and l == layers[0]:
                dma("sp", dbg_z, zb[:, :, 0:n], [("zb", c) for c in range(8)], ["dbg_z"], P.new_dma_sem("dbgd_a"))
                dma("sp", dbg_pm, msb[:, :, 0:n], [("msb", g) for g in range(4)], ["dbg_pm"], P.new_dma_sem("dbgd_b"))
                dma("sp", dbg_mg, mg[:, :, 0:n], [("mg", c) for c in range(8)], ["dbg_mg"], P.new_dma_sem("dbgd_c"))
            c_o = {}
            for m in range(8):
                wv, wk, cs_ = wchunk("wo", l, 0, m, c_o)
                ps, pk = next_ps()
                mm(ps[:, 0:n], pk, [(wv[:, c, cs_], mg[:, c, 0:n]) for c in range(8)], [wk] + [("mg", c) for c in range(8)])
                dve(lambda e, ps=ps, m=m, t0=t0, n=n: e.scalar_tensor_tensor(
                    out=seg.X[:, m, t0:t0 + n], in0=ps[:, 0:n], scalar=dv[:, seg.col, 2, m:m + 1],
                    in1=seg.X[:, m, t0:t0 + n], op0=ALU.mult, op1=ALU.add),
                    [pk, ("dv", seg.col), xkey(seg, m, ti)], [xkey(seg, m, ti)])
            if debug and is_lat and ti == len(seg.tiles) - 1 and l == layers[0]:
                dsem2 = P.new_dma_sem("dbgdump2")
                dma("sp", dbg_xa, seg.X[:, :, t0:t0 + n], [xkey(seg, m, ti) for m in range(8)], ["dbg_xa"], dsem2)
            norm_mod(seg, ti, 1, mg, "mg")
            zk = [("mg", kc) for kc in range(8)]
            c_1 = {}

            def w1_part(j0, t0=t0, n=n, zk=zk, c_1=c_1):
                for j in range(j0, j0 + 4):
                    wv, wk, cs_ = wchunk("w1", l, 0, j, c_1)
                    ps, pk = next_ps()
                    mm(ps[:, 0:n], pk, [(wv[:, kc, cs_], mg[:, kc, 0:n]) for kc in range(8)], [wk] + zk)
                    r_ = nxt("relu", 2)
                    act(relu[:, r_, 0:n], ps[:, 0:n], AF.Relu, [pk], [("relu", r_)])
                    dve(lambda e, r_=r_, j=j, n=n: e.tensor_tensor(
                        out=HID[:, j, 0:n], in0=relu[:, r_, 0:n], in1=relu[:, r_, 0:n], op=ALU.mult),
                        [("relu", r_)], [bigkey(("hid", j))])

            def w2_part(m, t0=t0, n=n, ti=ti):
                w2a = wblock("w2", l, 2 * m)
                w2b = wblock("w2", l, 2 * m + 1)
                ps, pk = next_ps()
                terms = [(w2a[0][:, j, :], HID[:, j, 0:n]) for j in range(16)] + \
                        [(w2b[0][:, j, :], HID[:, 16 + j, 0:n]) for j in range(16)]
                mm(ps[:, 0:n], pk, terms, [w2a[1], w2b[1]] + [("hid", j) for j in range(32)])
                dve(lambda e, ps=ps, m=m, t0=t0, n=n: e.scalar_tensor_tensor(
                    out=seg.X[:, m, t0:t0 + n], in0=ps[:, 0:n], scalar=mod[:, 40 + m, seg.col:seg.col + 1],
                    in1=seg.X[:, m, t0:t0 + n], op0=ALU.mult, op1=ALU.add),
                    [pk, ("mod", seg.col), xkey(seg, m, ti)], [xkey(seg, m, ti)])

            for j0 in range(0, 32, 4):
                parts.append(lambda j0=j0: w1_part(j0))
            for m in range(8):
                parts.append(lambda m=m: w2_part(m))
        return parts

    out_sems = [P.new_dma_sem(f"outst{i}") for i in range(2)]

    def store_seg(seg, dst, do_norm):
        for ti, (t0, n) in enumerate(seg.tiles):
            nsub = n // 128
            if do_norm:
                rms_stats(seg, ti)
            for c in range(8):
                if do_norm:
                    dve(lambda e, c=c, t0=t0, n=n: e.scalar_tensor_tensor(
                        out=YN[:, c, 0:n], in0=seg.X[:, c, t0:t0 + n], scalar=vg[:, VG_FG + c:VG_FG + c + 1],
                        in1=rstd[:, 0:n], op0=ALU.mult, op1=ALU.mult),
                        [xkey(seg, c, ti), "rstd", "vg"], [bigkey(("yn", c))])
                else:
                    dve(lambda e, c=c, t0=t0, n=n: e.tensor_copy(out=YN[:, c, 0:n], in_=seg.X[:, c, t0:t0 + n]),
                        [xkey(seg, c, ti)], [bigkey(("yn", c))])
            for sbi in range(nsub):
                so = nxt("stage", 2)
                for half in range(2):
                    ps, pk = next_ps()

                    def fn(e, ps=ps, half=half, sbi=sbi):
                        ins = None
                        for cc in range(4):
                            c = half * 4 + cc
                            ins = e.transpose(ps[:, cc * 128:(cc + 1) * 128], YN[:, c, sbi * 128:(sbi + 1) * 128], ident[:])
                        return ins
                    P.add("pe", fn, reads=[("yn", c) for c in range(8)] + ["ident"], writes=[pk])
                    if half == 0:
                        dve(lambda e, ps=ps, so=so: e.tensor_copy(out=stage[:, so, 0:512], in_=ps[:, :]),
                            [pk], [bigkey(("stage", so, 0))])
                    else:
                        P.add("act", lambda e, ps=ps, so=so: e.copy(out=stage[:, so, 512:1024], in_=ps[:, :]),
                              reads=[pk], writes=[bigkey(("stage", so, 1))])
                dma("sp", dst[t0 + sbi * 128:t0 + (sbi + 1) * 128, :], stage[:, so, :], [("stage", so, 0), ("stage", so, 1)],
                    [("outrow", seg.name, ti, sbi)], out_sems[so])

    cst = sb("cst", [128, 2])
    dve(lambda e: e.memset(cst[:, 0:1], EPS), [], ["cst"])
    dve(lambda e: e.memset(cst[:, 1:2], 0.25), ["cst"], ["cst"])
    vg_eps = cst[:, 0:1]
    vg_q = cst[:, 1:2]

    load_seg(ctx, c_in)
    load_seg(lat, x_in)
    for c in range(8):
        for ti in range(len(lat.tiles)):
            ukey(lat, c, ti)
        bigkey(("yn", c))
    for j in range(32):
        bigkey(("hid", j))
    for so in range(2):
        for hf in range(2):
            bigkey(("stage", so, hf))
    bigkey(("Uhalo", "lat"))
    bigkey(("UhaloR", "lat"))
    try:
        chk("load")
        for li, l in enumerate(layers):
            last = (l == DEPTH - 1)
            layer_setup(l)
            chk("setup")
            build_convd()
            load_gw(l, 0)
            if li == 0:
                drain(prep_weights(l, "early"), 999)
            chk("prep")
            big_barrier()
            dve(lambda e: e.memset(U[:, :, 0:2], 0.0), [], [("Uhalo", "lat")])
            dve(lambda e: e.memset(CAR[:], 0.0), [], [("car", i, c) for i in range(3) for c in range(8)])
            phase1a(ctx, l, not last)
            chk("p1a_ctx")
            cparts = phase1b(ctx, l, not last, as_parts=True)
            lparts = phase1a(lat, l, True, as_parts=True)
            while cparts or lparts:
                drain(cparts, 1)
                drain(lparts, 1)
            chk("p1a_lat")
            dve(lambda e: e.tensor_copy(out=exch[:, :].rearrange("p (c k) -> p c k", k=2), in_=U[:, :, T:T + 2]),
                [ukey(lat, c, len(lat.tiles) - 1) for c in range(8)] + ["exch_in"], ["exch"])
            exchange()
            if li == 0:
                drain(prep_weights(l, "late"), 999)
            nlt = len(lat.tiles)
            phase1b(lat, l, True, tiles=list(range(nlt - 1)))
            exchange_finish()
            selv = selrow[:, :].rearrange("p (c k) -> p c k", k=2)
            for k in range(2):
                dve(lambda e, k=k: e.tensor_copy(out=U[:, :, T + 2 + k], in_=selv[:, :, 1 - k]),
                    ["selrow"], [("UhaloR", "lat")])
            chk("halo")
            phase1b(lat, l, True, tiles=[nlt - 1])
            chk("p1_lat")
            dve(lambda e: e.tensor_copy(out=exch[:, 0:8], in_=CAR[:, 0, :]),
                [("car", 0, c) for c in range(8)] + ["exch_in"], ["exch"])
            exchange()

            def apply_carry():
                exchange_finish()
                dve(lambda e: e.tensor_copy(out=CAR[:, 1, :], in_=selrow[:, 0:8]), ["selrow"],
                    [("car", 1, c) for c in range(8)])
            big_barrier()
            load_gw(l, 1)
            nextprep = prep_weights(layers[li + 1]) if li + 1 < len(layers) else []
            chk("carry")
            pending = []
            pre = [apply_carry]
            for sg, ti in [(lat, t_) for t_ in reversed(range(len(lat.tiles)))] + ([] if last else [(ctx, 0)]):
                if not (sg is lat and ti == len(lat.tiles) - 1):
                    drain(nextprep, 20)
                pending = phase2_tile(sg, l, ti, pending, pre)
            drain(pending, 99)
            drain(nextprep, 999)
            chk("layer%d" % l)
    except _Stop:
        pass
    big_barrier()
    if final:
        store_seg(lat, out_d, True)
    else:
        store_seg(lat, out_d, False)
        store_seg(ctx, outc_d, False)
    P.add("sp", lambda e: e.nop(), reads=[k for k in list(P.last_w.keys()) if isinstance(k, tuple) and k[0] == "outrow"],
          writes=["done"])
    P.emit(nc, st)
    st.close()
    return nc


def _fm(v):
    v = np.asarray(v, np.float32)
    return np.ascontiguousarray(v.reshape(-1, 128).T)


def _pool_matrix(L, w):
    t = np.arange(L)
    lo = np.clip(t - w // 2, 0, L)
    hi = np.clip(t + w - w // 2, 0, L)
    M = np.zeros((L, L), np.float32)
    for i in range(L):
        M[i, lo[i]:hi[i]] = np.float32(1.0) / np.float32(hi[i] - lo[i])
        M[i, i] -= 1.0
    return M


def _structural_constants(flip):
    wins = (2, 4, 8, 16)
    pm_lat = np.zeros((128, 4, 128), np.float32)
    pm_ctx = np.zeros((128, 4, 2, 256), np.float32)
    for g, w in enumerate(wins):
        M = _pool_matrix(64, w)
        Mc = _pool_matrix(256, w)
        if flip:
            M = M[::-1, ::-1]
            Mc = Mc[::-1, ::-1]
        blk = np.zeros((128, 128), np.float32)
        blk[:64, :64] = M
        blk[64:, 64:] = M
        pm_lat[:, g, :] = blk.T
        McT = np.ascontiguousarray(Mc.T)
        pm_ctx[:, g, 0, :] = McT[0:128]
        pm_ctx[:, g, 1, :] = McT[128:256]
    return pm_lat, pm_ctx


def _prepare(inputs):
    f = lambda k: np.asarray(inputs[k], np.float32)
    x, c, ctx, c_ctx = f("x"), f("c"), f("ctx"), f("c_ctx")
    shared = {
        "ident": np.eye(128, dtype=np.float32),
        "w_ada": f("w_ada"), "w_in": f("w_in"), "w_lru_out": f("w_lru_out"), "w_pool_out": f("w_pool_out"),
        "w_o": f("w_o"), "mlp_w1": f("mlp_w1"), "mlp_w2": f("mlp_w2"),
        "pool_w": np.ascontiguousarray(f("pool_w").transpose(0, 2, 1, 3)),
    }
    conv_w, conv_b = f("conv_w"), f("conv_b")
    w_r, w_i, b_r, b_i, lam = f("lru_w_r"), f("lru_w_i"), f("lru_b_r"), f("lru_b_i"), f("lru_lambda")
    per_half = []
    for h in range(2):
        dirs = (0, 1) if h == 0 else (1, 0)
        vec_l = np.zeros((DEPTH, 128, V_PER_LAYER), np.float32)
        gw = np.zeros((DEPTH, 128, 4, 8, 128), np.float32)
        for l in range(DEPTH):
            vec_l[l, :, V_N1G:V_N1G + 8] = _fm(inputs["norm1_g"][l])
            vec_l[l, :, V_N2G:V_N2G + 8] = _fm(inputs["norm2_g"][l])
            vec_l[l, :, V_CB:V_CB + 8] = _fm(conv_b[l])
            taps = np.zeros((5, D), np.float32)
            if h == 0:
                taps[0:4] = conv_w[l]
            else:
                taps[1:5] = conv_w[l][::-1]
            for j in range(5):
                vec_l[l, :, V_CW + j:V_CW + 40:5] = _fm(taps[j])
            for dl, d in enumerate(dirs):
                vec_l[l, :, V_BR + 8 * dl:V_BR + 8 * dl + 8] = _fm(b_r[l, d])
                vec_l[l, :, V_BI + 8 * dl:V_BI + 8 * dl + 8] = _fm(b_i[l, d])
                vec_l[l, :, V_LAM + 8 * dl:V_LAM + 8 * dl + 8] = _fm(lam[l, d])
                for gi, wsrc in enumerate((w_r, w_i)):
                    for ch in range(8):
                        for hh in range(2):
                            gw[l, hh * 64:(hh + 1) * 64, 2 * dl + gi, ch, hh * 64:(hh + 1) * 64] = wsrc[l, d, 2 * ch + hh]
            vec_l[l, :, V_PS:V_PS + 4] = _fm(inputs["pool_scale"][l])
            vec_l[l, :, V_BADA:V_BADA + 48] = _fm(inputs["b_ada"][l])
        pm_lat, pm_ctx = _structural_constants(h == 1)
        per_half.append(dict(vec_l=vec_l, gw=gw, pm_lat=pm_lat, pm_ctx=pm_ctx))
    in_maps = []
    for core in range(8):
        b, h = core // 2, core % 2
        xs = x[b, h * T:(h + 1) * T]
        cs = ctx[b]
        if h == 1:
            xs = xs[::-1]
            cs = cs[::-1]
        vec_g = np.zeros((128, V_GLOBAL), np.float32)
        vec_g[:, VG_FG:VG_FG + 8] = _fm(inputs["final_g"])
        vec_g[:, VG_C:VG_C + 8] = _fm(c[b])
        vec_g[:, VG_CC:VG_CC + 8] = _fm(c_ctx)
        vec_g[:, VG_SEL + (1 - h)] = 1.0
        m = dict(shared)
        m.update(per_half[h])
        m["x_in"] = np.ascontiguousarray(xs)
        m["c_in"] = np.ascontiguousarray(cs)
        m["vec_g"] = vec_g
        in_maps.append(m)
    return in_maps


_NC_CACHE = {}


def kernel(**inputs):
    in_maps = _prepare(inputs)
    key = ("full",)
    if key not in _NC_CACHE:
        _NC_CACHE[key] = build_nc((0, 1), True)
    res = run_bass_kernel_spmd(_NC_CACHE[key], in_maps, core_ids=list(range(8)))
    out = np.zeros((NB, SEQ, D), np.float32)
    for core in range(8):
        b, h = core // 2, core % 2
        o = res.results[core]["out"]
        if h == 1:
            o = o[::-1]
        out[b, h * T:(h + 1) * T] = o
    return out
```
